# Optimizing a Trainium2 kernel written in Bass

```python
import jax, jax.numpy as jnp
from jax import lax
import numpy as np

D_MODEL = 1024
BATCH = 4
SEQ = 4096
DEPTH = 2

N_BRANCH = 3
POOL_WINDOWS = (2, 4, 8, 16)
POOL_WIDTH = D_MODEL // 2
POOL_GROUP = POOL_WIDTH // len(POOL_WINDOWS)
CONV_WIDTH = D_MODEL // 2
CONV_K = 3
HEAD_DIM = 64
ATTN_GROUPS = ((128, 1), (512, 4), (2048, 16))
HEADS_PER_GROUP = 4
N_HEADS = HEADS_PER_GROUP * len(ATTN_GROUPS)
ATTN_WIDTH = N_HEADS * HEAD_DIM
ATTN_OUT = HEADS_PER_GROUP * HEAD_DIM
ATTN_BLOCK = 128
D_FF = 4 * D_MODEL
EPS = 1e-6
MASK_VALUE = -1e30

OFF_POOL = 0
OFF_CONV_B = OFF_POOL + POOL_WIDTH
OFF_CONV_C = OFF_CONV_B + CONV_WIDTH
OFF_CONV_X = OFF_CONV_C + CONV_WIDTH
OFF_Q = OFF_CONV_X + CONV_WIDTH
OFF_K = OFF_Q + ATTN_WIDTH
OFF_V = OFF_K + ATTN_WIDTH
OFF_GATE = OFF_V + ATTN_WIDTH
IN_COLS = OFF_GATE + N_BRANCH * D_MODEL

kernel_name = "hybrid_pool_conv_dilated_attn_block"


def rms_norm(x, gain):
    xf = x.astype(jnp.float32)
    y = xf * lax.rsqrt(jnp.mean(xf * xf, axis=-1, keepdims=True) + EPS)
    return (y * gain.astype(jnp.float32)).astype(x.dtype)


def multiscale_pool(u, w_mix, scale):
    B, S, _ = u.shape
    cs = jnp.cumsum(u.astype(jnp.float32), axis=1)
    pos = jnp.arange(S)
    diffs = []
    for g, w in enumerate(POOL_WINDOWS):
        c = cs[..., g * POOL_GROUP:(g + 1) * POOL_GROUP]
        lag = jnp.pad(c, ((0, 0), (w, 0), (0, 0)))[:, :S]
        count = jnp.minimum(pos + 1, w).astype(jnp.float32)[None, :, None]
        diffs.append((c - lag) / count - u[..., g * POOL_GROUP:(g + 1) * POOL_GROUP].astype(jnp.float32))
    d = jnp.stack(diffs, axis=2).astype(u.dtype)
    y = jnp.einsum('bsgc,gcd->bsgd', d, w_mix).reshape(B, S, POOL_WIDTH)
    return y * scale


def short_gated_conv(b_gate, c_gate, xin, conv_w):
    S = xin.shape[1]
    u = c_gate * xin
    up = jnp.pad(u, ((0, 0), (CONV_K - 1, 0), (0, 0)))
    y = conv_w[CONV_K - 1] * up[:, CONV_K - 1:CONV_K - 1 + S]
    for j in range(CONV_K - 1):
        y = y + conv_w[j] * up[:, j:j + S]
    return b_gate * y


def dilated_window_attention(q, k, v, window, dilation):
    B, S, H, Dh = q.shape
    span = window // dilation
    assert span <= ATTN_BLOCK
    L = S // dilation
    nb = -(-L // ATTN_BLOCK)
    Lp = nb * ATTN_BLOCK

    def fold(t):
        t = t.reshape(B, L, dilation, H, Dh).transpose(0, 2, 3, 1, 4)
        t = jnp.pad(t, ((0, 0), (0, 0), (0, 0), (0, Lp - L), (0, 0)))
        return t.reshape(B, dilation, H, nb, ATTN_BLOCK, Dh)

    def with_prev(t):
        prev = jnp.pad(t, ((0, 0), (0, 0), (0, 0), (1, 0), (0, 0), (0, 0)))[:, :, :, :nb]
        return jnp.concatenate([prev, t], axis=4)

    qb = fold(q).astype(jnp.float32)
    kc = with_prev(fold(k)).astype(jnp.float32)
    vc = with_prev(fold(v)).astype(jnp.float32)
    s = jnp.einsum('bdhnqc,bdhnkc->bdhnqk', qb, kc) * (Dh ** -0.5)
    qi = jnp.arange(ATTN_BLOCK)[:, None]
    ki = jnp.arange(2 * ATTN_BLOCK)[None, :] - ATTN_BLOCK
    rel = qi - ki
    band = (rel >= 0) & (rel <= span)
    has_prev = (jnp.arange(nb) > 0)[:, None, None] | (ki >= 0)[None]
    mask = band[None] & has_prev
    s = jnp.where(mask, s, MASK_VALUE)
    m = jnp.max(s, axis=-1, keepdims=True)
    p = jnp.exp(s - m)
    den = jnp.sum(p, axis=-1, keepdims=True)
    o = jnp.einsum('bdhnqk,bdhnkc->bdhnqc', p, vc) / den
    lse = (m + jnp.log(den))[..., 0]
    o = o.reshape(B, dilation, H, Lp, Dh)[:, :, :, :L].transpose(0, 3, 1, 2, 4).reshape(B, S, H, Dh)
    lse = lse.reshape(B, dilation, H, Lp)[..., :L].transpose(0, 3, 1, 2).reshape(B, S, H)
    return o.astype(q.dtype), lse


def dilated_mixture_attention(q, k, v):
    B, S = q.shape[:2]
    outs, lses = [], []
    for g, (window, dilation) in enumerate(ATTN_GROUPS):
        hs = slice(g * HEADS_PER_GROUP, (g + 1) * HEADS_PER_GROUP)
        o, lse = dilated_window_attention(q[:, :, hs], k[:, :, hs], v[:, :, hs], window, dilation)
        outs.append(o)
        lses.append(lse)
    wts = jax.nn.softmax(jnp.stack(lses, axis=0), axis=0)
    o = jnp.sum(wts[..., None] * jnp.stack(outs, axis=0).astype(jnp.float32), axis=0)
    return o.reshape(B, S, ATTN_OUT).astype(q.dtype)


def hybrid_layer(x, norm_mix, w_in, b_gate, pool_mix, pool_scale, conv_w, q_gain, k_gain,
                 w_pool_up, w_conv_out, w_attn_up, w_o, norm_mlp, w_ff1, w_ff2):
    B, S, D = x.shape
    h = rms_norm(x, norm_mix)
    z = jnp.einsum('bsd,dc->bsc', h, w_in)
    y_pool = multiscale_pool(z[..., OFF_POOL:OFF_CONV_B], pool_mix, pool_scale)
    y_conv = short_gated_conv(z[..., OFF_CONV_B:OFF_CONV_C], z[..., OFF_CONV_C:OFF_CONV_X],
                              z[..., OFF_CONV_X:OFF_Q], conv_w)
    q = rms_norm(z[..., OFF_Q:OFF_K].reshape(B, S, N_HEADS, HEAD_DIM), q_gain)
    k = rms_norm(z[..., OFF_K:OFF_V].reshape(B, S, N_HEADS, HEAD_DIM), k_gain)
    v = z[..., OFF_V:OFF_GATE].reshape(B, S, N_HEADS, HEAD_DIM)
    y_attn = dilated_mixture_attention(q, k, v)
    gates = jax.nn.sigmoid((z[..., OFF_GATE:] + b_gate).astype(jnp.float32)).astype(x.dtype)
    gates = gates.reshape(B, S, N_BRANCH, D)
    merged = (gates[:, :, 0] * (y_pool @ w_pool_up)
              + gates[:, :, 1] * (y_conv @ w_conv_out)
              + gates[:, :, 2] * (y_attn @ w_attn_up))
    x = x + merged @ w_o
    h2 = rms_norm(x, norm_mlp)
    x = x + jnp.square(jax.nn.relu(h2 @ w_ff1)) @ w_ff2
    return x


def setup_inputs(seed: int = 0) -> dict:
    key = jax.random.key(seed)
    ks = jax.random.split(key, 17)
    L, D = DEPTH, D_MODEL

    def nrm(k, shape, fan_in):
        return jax.random.normal(k, shape, jnp.float32) * (fan_in ** -0.5)

    def gain(k, shape):
        return 1.0 + 0.02 * jax.random.normal(k, shape, jnp.float32)

    return {
        "x": jax.random.normal(ks[0], (BATCH, SEQ, D), jnp.float32),
        "norm_mix": gain(ks[1], (L, D)),
        "w_in": nrm(ks[2], (L, D, IN_COLS), D),
        "b_gate": 0.01 * jax.random.normal(ks[3], (L, N_BRANCH * D), jnp.float32),
        "pool_mix": nrm(ks[4], (L, len(POOL_WINDOWS), POOL_GROUP, POOL_GROUP), POOL_GROUP),
        "pool_scale": gain(ks[5], (L, POOL_WIDTH)),
        "conv_w": nrm(ks[6], (L, CONV_K, CONV_WIDTH), CONV_K),
        "q_gain": gain(ks[7], (L, HEAD_DIM)),
        "k_gain": gain(ks[8], (L, HEAD_DIM)),
        "w_pool_up": nrm(ks[9], (L, POOL_WIDTH, D), POOL_WIDTH),
        "w_conv_out": nrm(ks[10], (L, CONV_WIDTH, D), CONV_WIDTH),
        "w_attn_up": nrm(ks[11], (L, ATTN_OUT, D), ATTN_OUT),
        "w_o": nrm(ks[12], (L, D, D), D),
        "norm_mlp": gain(ks[13], (L, D)),
        "w_ff1": nrm(ks[14], (L, D, D_FF), D),
        "w_ff2": nrm(ks[15], (L, D_FF, D), D_FF),
    }


def reference(x, norm_mix, w_in, b_gate, pool_mix, pool_scale, conv_w, q_gain, k_gain,
              w_pool_up, w_conv_out, w_attn_up, w_o, norm_mlp, w_ff1, w_ff2):
    for l in range(DEPTH):
        x = hybrid_layer(x, norm_mix[l], w_in[l], b_gate[l], pool_mix[l], pool_scale[l], conv_w[l],
                         q_gain[l], k_gain[l], w_pool_up[l], w_conv_out[l], w_attn_up[l], w_o[l],
                         norm_mlp[l], w_ff1[l], w_ff2[l])
    return x
```

```python
import numpy as np
import concourse.bass as bass
import concourse.mybir as mybir
from concourse.bass_utils import run_bass_kernel_spmd

F32 = mybir.dt.float32
BF16 = mybir.dt.bfloat16
U8 = mybir.dt.uint8
ALU = mybir.AluOpType
AF = mybir.ActivationFunctionType
ESZ = {F32: 4, BF16: 2, U8: 1}

D = 1024
T = 2048
LBT = 2048
PRE = 16
NKC = 8
TCW = 512
NTC = 4
DIL = (1, 4, 16)
POOLW = (2, 4, 8, 16)
EPS = 1e-6
NEG = -30000.0
N_CORES = 8
DEPTH = 2
IN_COLS = 7424
NWC = 58
CI_POOL, CI_CB, CI_CC, CI_CX, CI_Q, CI_K, CI_V, CI_G = 0, 4, 8, 12, 16, 22, 28, 34
SP_NMIX, SP_NMLP, SP_BG, SP_PSC, SP_CW, SP_QG, SP_KG, SP_FLAG, SP_EPS = 0, 8, 16, 40, 44, 56, 57, 58, 59
NSP = 64
C_ID, C_ONES, C_BD, C_MN, C_MF = 0, 128, 256, 384, 896
NCST = 1408
LBBLK = (0, 1, 5)
NLBBLK = 21


def _region(ap):
    t = ap.tensor
    dims = list(ap.ap)
    pstep = dims[0][0]
    esz = ESZ[ap.dtype]
    off = ap.offset
    f0 = off % pstep if pstep > 0 else off
    ext = 1
    for s, c in dims[1:]:
        ext += (c - 1) * abs(s)
    if t.name.startswith("ps"):
        return (t.name, 0, 2048)
    if t.name.startswith("xch"):
        ext = 1
        for s_, c in dims:
            ext += (c - 1) * abs(s_)
        return (t.name, ap.offset, ap.offset + ext)
    return (t.name, f0 * esz, (f0 + ext) * esz)


class Rec:
    ENGS = ("pe", "act", "dve", "pool", "sp")

    def __init__(self):
        self.ops = []
        self.recs = {}

    def _access(self, reg, idx, eng, is_write, deps):
        name, lo, hi = reg
        lst = self.recs.setdefault(name, [])
        if is_write:
            keep = []
            for r in lst:
                if r[0] < hi and lo < r[1]:
                    deps.add(r[2])
                    if lo <= r[0] and r[1] <= hi:
                        continue
                keep.append(r)
            keep.append((lo, hi, idx, eng, True))
            self.recs[name] = keep
        else:
            keep = []
            for r in lst:
                if r[4]:
                    if r[0] < hi and lo < r[1]:
                        deps.add(r[2])
                    keep.append(r)
                else:
                    if r[3] == eng and eng not in ("sp", "pool") and lo <= r[0] and r[1] <= hi:
                        continue
                    keep.append(r)
            keep.append((lo, hi, idx, eng, False))
            self.recs[name] = keep

    def add(self, eng, fn, reads=(), writes=(), dma=False, cc=False):
        idx = len(self.ops)
        deps = set()
        for ap in reads:
            if ap is None:
                continue
            self._access(_region(ap), idx, eng, False, deps)
        for ap in writes:
            self._access(_region(ap), idx, eng, True, deps)
        deps.discard(idx)
        self.ops.append({"eng": eng, "fn": fn, "deps": deps, "dma": dma or cc, "inc": 1 if cc else 16,
                         "semkey": "cc" if cc else eng})
        return idx

    def emit(self, nc, sems, dma_sems, final_wait_eng="sp"):
        ops = self.ops
        signal = [False] * len(ops)
        for i, o in enumerate(ops):
            if o["dma"]:
                signal[i] = True
            for p in o["deps"]:
                if ops[p]["eng"] == "pe" and o["eng"] == "pe" and not ops[p]["dma"] and not o["dma"]:
                    continue
                signal[p] = True
        cnt = {e: 0 for e in self.ENGS}
        dcnt = {e: 0 for e in self.ENGS}
        duse = {}
        for i, o in enumerate(ops):
            if o["dma"]:
                pool = dma_sems[o["semkey"]]
                s = pool[dcnt.get(o["semkey"], 0) % len(pool)]
                dcnt[o["semkey"]] = dcnt.get(o["semkey"], 0) + 1
                duse[s] = duse.get(s, 0) + o["inc"]
                o["sem"] = s
                o["val"] = duse[s]
            elif signal[i]:
                cnt[o["eng"]] += 1
                o["sem"] = sems[o["eng"]]
                o["val"] = cnt[o["eng"]]
        by_eng = {e: [] for e in self.ENGS}
        for i, o in enumerate(ops):
            by_eng[o["eng"]].append(i)
        final_dma = [o for o in ops if o["dma"]]

        def run(eng_name, eng):
            waited = {}
            for i in by_eng[eng_name]:
                o = ops[i]
                w = {}
                for p in o["deps"]:
                    po = ops[p]
                    if po["eng"] == "pe" and eng_name == "pe" and not po["dma"] and not o["dma"]:
                        continue
                    s = po["sem"]
                    if w.get(s, (None, 0))[1] < po["val"]:
                        w[s] = (s, po["val"])
                if o["dma"] and o["val"] > o["inc"]:
                    s = o["sem"]
                    if w.get(s, (None, 0))[1] < o["val"] - o["inc"]:
                        w[s] = (s, o["val"] - o["inc"])
                for s, v in w.values():
                    if waited.get(s, 0) < v:
                        eng.wait_ge(s, v)
                        waited[s] = v
                inst = o["fn"](eng)
                if o["dma"]:
                    inst.then_inc(o["sem"], o["inc"])
                elif signal[i]:
                    inst.then_inc(o["sem"], 1)
            if eng_name == final_wait_eng:
                last = {}
                for o in final_dma:
                    if last.get(o["sem"], (None, 0))[1] < o["val"]:
                        last[o["sem"]] = (o["sem"], o["val"])
                for s_, v in last.values():
                    if waited.get(s_, 0) < v:
                        eng.wait_ge(s_, v)
                        waited[s_] = v

        with nc.Block() as block:
            @block.tensor
            def _(e):
                run("pe", e)

            @block.scalar
            def _(e):
                run("act", e)

            @block.vector
            def _(e):
                run("dve", e)

            @block.gpsimd
            def _(e):
                run("pool", e)

            @block.sync
            def _(e):
                run("sp", e)


class Arena:
    def __init__(self, tensor, size):
        self.t = tensor
        self.size = size

    def view(self, off, shape, dt):
        n = 1
        for s in shape:
            n *= s
        nb = n * ESZ[dt]
        assert off % 4 == 0 and off + nb <= self.size, (off, nb, self.size)
        v = self.t[:, off:off + nb].bitcast(dt)
        if len(shape) == 2:
            return v.rearrange("p (a b) -> p a b", a=shape[0])
        if len(shape) == 3:
            return v.rearrange("p (a b c) -> p a b c", a=shape[0], b=shape[1])
        return v


def build_program(n_layers, debug=False, phases="ABCDEFGkvqabcn"):
    nc = bass.Bass("TRN2", target_bir_lowering=False)
    L = n_layers
    dr = {}

    def din(name, shape):
        dr[name] = nc.dram_tensor(name, shape, F32, kind="ExternalInput").ap()
        return dr[name]

    xT_d = din("xT", [D, T])
    xlb_d = din("xlbT", [D, LBT])
    win_d = din("win", [L, NWC, 128, NKC * 128])
    pmix_d = din("pmix", [L, 128, 4 * 128])
    ups_d = din("ups", [L, 8, 128, 10 * 128])
    wo_d = din("wo", [L, 8, 128, NKC * 128])
    w1_d = din("w1", [L, 4, 128, NKC * 1024])
    w2_d = din("w2", [L, 4, 128, 8 * 1024])
    smallp_d = din("smallp", [L, 128, NSP])
    cst_d = din("cst", [128, NCST])
    invc_d = din("invc", [128, 64])
    out_d = nc.dram_tensor("outT", [D, T], F32, kind="ExternalOutput").ap()
    dbg = {}
    if debug:
        for nm, rows in (("d_yattn", 256), ("d_ypool", 512), ("d_yconv", 512), ("d_merged", 1024), ("d_h", 1024), ("d_x1", 1024)):
            dbg[nm] = nc.dram_tensor(nm, [rows, T], F32, kind="ExternalOutput").ap()

    ARENA = 207 * 1024
    R = Rec()
    import contextlib
    with contextlib.ExitStack() as es:
        arena_t = es.enter_context(nc.sbuf_tensor("arena", [128, ARENA], U8))
        A = Arena(arena_t, ARENA)
        ps = [es.enter_context(nc.psum_tensor(f"ps{i}", [128, 512], F32)) for i in range(8)]
        sems = {e: es.enter_context(nc.semaphore(f"s_{e}")) for e in ("pe", "act", "dve", "pool", "sp")}
        dma_sems = {
            "pool": [es.enter_context(nc.semaphore(f"dp{i}")) for i in range(12)],
            "sp": [es.enter_context(nc.semaphore(f"ds{i}")) for i in range(8)],
            "cc": [es.enter_context(nc.semaphore(f"dc{i}")) for i in range(4)],
        }
        xch_in = [[nc.dram_tensor(f"xch_in_{l}_{c}", [D, 512], BF16) for c in range(4)] for l in range(L - 1)]
        xch_out = [[nc.dram_tensor(f"xch_out_{l}_{c}", [2 * D, 512], BF16) for c in range(4)] for l in range(L - 1)]

        off = 0

        def alloc(nbytes):
            nonlocal off
            o = off
            off += (nbytes + 63) // 64 * 64
            return o

        o_xT = alloc(8 * T * 4)
        o_hT = alloc(8 * (PRE + T) * 2)
        o_cst = alloc(NCST * 2)
        o_invc = alloc(64 * 4)
        o_smallp = alloc(L * NSP * 4)
        NSLOT = 5
        o_wslot = alloc(NSLOT * 8 * 128 * 2)
        o_pm = alloc(4 * 128 * 2)
        o_yattn = alloc(2 * T * 2)
        o_ph = off
        o_ktlb = alloc(2 * 2688 * 2)
        o_vlb = alloc(2 * NLBBLK * 192 * 2)
        o_vt = alloc(T * 2)
        o_sq = alloc(2 * 512 * 2)
        o_lnv = alloc(2 * 512 * 4)
        o_rstd = alloc(2 * 512 * 4)
        o_alias = off
        o_acc = alloc(2 * T * 4)
        o_ktown = alloc(T * 2)
        o_vown = alloc(16 * 192 * 2)
        o_qtf = alloc(2 * T * 2)
        o_pt = alloc(4 * 512 * 2)
        o_rtmp = alloc(2 * 512 * 4)
        end_c = off
        off = o_alias
        o_xlb = alloc(2 * 8 * 256 * 4)
        o_hlb = alloc(2 * 8 * 256 * 2)
        o_sqb = alloc(8 * 512 * 2)
        assert off <= end_c
        off = end_c
        o_lnvA = alloc(512 * 4)
        o_rstdA = alloc(512 * 4)
        o_rstdB = alloc(512 * 4)
        assert off <= ARENA, off
        o_vtlb_small = o_alias + 32768
        assert o_vtlb_small + 2560 <= end_c and o_sqb + 8 * 512 * 2 <= o_vtlb_small
        off = o_ph
        o_ypool = alloc(4 * T * 2)
        o_yconv = alloc(4 * T * 2)
        o_mix = off
        o_U = alloc((PRE + T) * 4)
        o_SAl = alloc(2 * 528 * 4)
        o_SBl = alloc(2 * 528 * 4)
        o_Dch = alloc(2 * 512 * 2)
        o_dt = alloc(16 * 4)
        assert off <= ARENA
        off = o_mix
        o_UC = alloc((PRE + T) * 4)
        o_CSl = alloc(2 * 528 * 4)
        o_Ych = alloc(2 * 512 * 4)
        assert off <= ARENA
        off = o_mix
        o_merged = alloc(8 * T * 2)
        o_upsl = alloc(2 * 10 * 128 * 2)
        o_sg = alloc(3 * 512 * 4)
        o_mt = alloc(512 * 4)
        o_mt2 = alloc(512 * 4)
        assert off <= ARENA, off
        o_dbg = ARENA - T * 4
        if debug:
            assert off <= o_dbg, (off, o_dbg)
        off = o_yattn
        o_w1 = alloc(2 * 8 * 1024 * 2)
        o_w2 = alloc(2 * 8 * 1024 * 2)
        o_fT = alloc(2 * 8 * 512 * 2)
        o_sqb2 = alloc(8 * 512 * 2)
        o_relu = alloc(2 * 512 * 4)
        o_lnv2 = alloc(512 * 4)
        o_rstd2 = alloc(512 * 4)
        assert off <= ARENA, off

        xT = A.view(o_xT, [8, T], F32)
        hT = A.view(o_hT, [8, PRE + T], BF16)
        cst = A.view(o_cst, [NCST], BF16)
        ident = cst[:, C_ID:C_ID + 128]
        ones_m = cst[:, C_ONES:C_ONES + 128]
        bd_m = cst[:, C_BD:C_BD + 128]
        mask_n = cst[:, C_MN:C_MN + 512]
        mask_f = cst[:, C_MF:C_MF + 512]
        invc = A.view(o_invc, [4, 16], F32)
        smallp = A.view(o_smallp, [L, NSP], F32)
        wslots = [A.view(o_wslot + i * 2048, [8, 128], BF16) for i in range(NSLOT)]
        yattn = A.view(o_yattn, [2, T], BF16)
        ypool = A.view(o_ypool, [4, T], BF16)
        yconv = A.view(o_yconv, [4, T], BF16)
        ktlb = A.view(o_ktlb, [2, 2688], BF16)
        vlb = A.view(o_vlb, [2, NLBBLK, 192], BF16)
        vt = A.view(o_vt, [T], BF16)
        sqs = [A.view(o_sq + i * 1024, [512], BF16) for i in range(2)]
        lnvs = [A.view(o_lnv + i * 2048, [512], F32) for i in range(2)]
        rstds = [A.view(o_rstd + i * 2048, [512], F32) for i in range(2)]
        lnvA = A.view(o_lnvA, [512], F32)
        rstdA = A.view(o_rstdA, [512], F32)
        rstdB = A.view(o_rstdB, [512], F32)
        acc = A.view(o_acc, [2, T], F32)
        ktown = A.view(o_ktown, [T], BF16)
        vown = A.view(o_vown, [16, 192], BF16)
        qtf2 = A.view(o_qtf, [2, T], BF16)
        pts = [A.view(o_pt + i * 1024, [512], BF16) for i in range(4)]
        rtmps = [A.view(o_rtmp + i * 2048, [512], F32) for i in range(2)]
        xlbs = [A.view(o_xlb + i * 8192, [8, 256], F32) for i in range(2)]
        hlbs = [A.view(o_hlb + i * 4096, [8, 256], BF16) for i in range(2)]
        sqb = A.view(o_sqb, [8, 512], BF16)
        hlbs512 = [A.view(o_xlb + i * 8192, [8, 512], BF16) for i in range(2)]
        Ubuf = A.view(o_U, [PRE + T], F32)
        SAl = [A.view(o_SAl + i * 2112, [528], F32) for i in range(2)]
        SBl = [A.view(o_SBl + i * 2112, [528], F32) for i in range(2)]
        Dch = [A.view(o_Dch + i * 1024, [512], BF16) for i in range(2)]
        pm = A.view(o_pm, [4, 128], BF16)
        dtmp = A.view(o_dt, [16], F32)
        UC = A.view(o_UC, [PRE + T], F32)
        CSl = [A.view(o_CSl + i * 2112, [528], F32) for i in range(2)]
        Ych = [A.view(o_Ych + i * 2048, [512], F32) for i in range(2)]
        merged = A.view(o_merged, [8, T], BF16)
        upsl = [A.view(o_upsl + i * 2560, [10, 128], BF16) for i in range(2)]
        sg = A.view(o_sg, [3, 512], F32)
        mt = A.view(o_mt, [512], F32)
        mt2 = A.view(o_mt2, [512], F32)
        w1s = [A.view(o_w1 + i * 16384, [8, 1024], BF16) for i in range(2)]
        w2s = [A.view(o_w2 + i * 16384, [8, 1024], BF16) for i in range(2)]
        fTs = [A.view(o_fT + i * 8192, [8, 512], BF16) for i in range(2)]
        relus = [A.view(o_relu + i * 2048, [512], F32) for i in range(2)]
        sqb2 = A.view(o_sqb2, [8, 512], BF16)
        lnv2 = A.view(o_lnv2, [512], F32)
        rstd2 = A.view(o_rstd2, [512], F32)

        bank_ctr = [0]

        def bank():
            b = ps[bank_ctr[0] % 8][:, :]
            bank_ctr[0] += 1
            return b

        def mm(out, lhsT, rhs, start, stop, skip=False):
            R.add("pe", lambda e, out=out, lhsT=lhsT, rhs=rhs, start=start, stop=stop, skip=skip:
                  e.matmul(out, lhsT, rhs, start=start, stop=stop, skip_group_check=skip),
                  reads=[lhsT, rhs], writes=[out])

        def act(out, in_, func, bias=None, scale=None, extra_reads=()):
            kw = {}
            rd = [in_] + list(extra_reads)
            if bias is not None:
                kw["bias"] = bias
                if not isinstance(bias, float):
                    rd.append(bias)
            if scale is not None:
                kw["scale"] = scale
                if not isinstance(scale, float):
                    rd.append(scale)
            R.add("act", lambda e, out=out, in_=in_, func=func, kw=kw: e.activation(out, in_, func, **kw),
                  reads=rd, writes=[out])

        def dve_tt(out, in0, in1, op):
            R.add("dve", lambda e, out=out, in0=in0, in1=in1, op=op: e.tensor_tensor(out, in0, in1, op),
                  reads=[in0, in1], writes=[out])

        def gp_tt(out, in0, in1, op):
            R.add("pool", lambda e, out=out, in0=in0, in1=in1, op=op: e.tensor_tensor(out, in0, in1, op),
                  reads=[in0, in1], writes=[out])

        def dve_stt(out, in0, scalar, in1, op0, op1):
            rd = [in0, in1] + ([] if isinstance(scalar, float) else [scalar])
            R.add("dve", lambda e, out=out, in0=in0, scalar=scalar, in1=in1, op0=op0, op1=op1:
                  e.scalar_tensor_tensor(out, in0, scalar, in1, op0, op1), reads=rd, writes=[out])

        def dve_ts(out, in0, scalar1, op0):
            rd = [in0] + ([] if isinstance(scalar1, float) else [scalar1])
            R.add("dve", lambda e, out=out, in0=in0, scalar1=scalar1, op0=op0:
                  e.tensor_scalar(out, in0, scalar1, None, op0), reads=rd, writes=[out])

        def dve_copy(out, in_):
            R.add("dve", lambda e, out=out, in_=in_: e.tensor_copy(out, in_), reads=[in_], writes=[out])

        def dve_recip(out, in_):
            R.add("dve", lambda e, out=out, in_=in_: e.reciprocal(out, in_), reads=[in_], writes=[out])

        def dve_memset(out, val):
            R.add("dve", lambda e, out=out, val=val: e.memset(out, val), reads=[], writes=[out])

        def dma(eng, out, in_, reads=(), writes=()):
            R.add(eng, lambda e, out=out, in_=in_: e.dma_start(out=out, in_=in_), reads=list(reads), writes=list(writes), dma=True)

        wslot_ctr = [0]
        prefetched = []

        def _issue_wchunk(src):
            s_ = wslots[wslot_ctr[0] % NSLOT]
            wslot_ctr[0] += 1
            dma("pool", s_, src.rearrange("p (k n) -> p k n", k=8), writes=[s_])
            return s_

        def load_wchunk(src, key=None):
            if prefetched:
                k0, s0_ = prefetched.pop(0)
                assert k0 == key, (k0, key)
                return s0_
            return _issue_wchunk(src)

        wcache = {}

        def load_win(l, ci):
            key = ("win", l, ci)
            if prefetched:
                return load_wchunk(win_d[l, ci], key=key)
            s_ = load_wchunk(win_d[l, ci], key=key)
            wcache[key] = (s_, wslot_ctr[0] - 1)
            return s_

        def load_win_cached(l, ci):
            key = ("win", l, ci)
            if not prefetched and key in wcache and wcache[key][1] is not None and wslot_ctr[0] - wcache[key][1] < NSLOT:
                return wcache[key][0]
            return load_win(l, ci)

        def prefetch_win(l, cis):
            assert not prefetched
            for ci in cis:
                s_ = _issue_wchunk(win_d[l, ci])
                prefetched.append((("win", l, ci), s_))
                wcache[("win", l, ci)] = (s_, wslot_ctr[0] - 1)

        dma("pool", cst, cst_d, writes=[cst])
        dma("sp", invc, invc_d.rearrange("p (g j) -> p g j", g=4), writes=[invc])
        for l in range(L):
            dma("sp", smallp[:, l, :], smallp_d[l], writes=[smallp[:, l, :]])
        def load_x(tc):
            for c in range(8):
                dma("pool", xT[:, c, tc * TCW:(tc + 1) * TCW], xT_d[c * 128:(c + 1) * 128, tc * TCW:(tc + 1) * TCW],
                    writes=[xT[:, c, tc * TCW:(tc + 1) * TCW]])
        load_x(0)

        def spc(l, col):
            return smallp[:, l, col:col + 1]

        def rms_stats(src_fn, ntok, l, sqbuf, lnbuf, rsbuf, flag=False):
            for kc in range(8):
                act(sqbuf[:, kc, 0:ntok], src_fn(kc), AF.Square)
            b = bank()
            for kc in range(8):
                mm(b[:, 0:ntok], ones_m, sqbuf[:, kc, 0:ntok], kc == 0, kc == 7)
            act(lnbuf[:, 0:ntok], b[:, 0:ntok], AF.Ln, bias=spc(l, SP_EPS))
            act(rsbuf[:, 0:ntok], lnbuf[:, 0:ntok], AF.Exp, scale=-0.5)
            if flag:
                dve_ts(rsbuf[:, 0:ntok], rsbuf[:, 0:ntok], spc(l, SP_FLAG), ALU.mult)

        def rms_apply(src_fn, dst_fn, ntok, l, gain_col, rsbuf):
            for kc in range(8):
                dve_stt(dst_fn(kc), src_fn(kc), spc(l, gain_col + kc), rsbuf[:, 0:ntok], ALU.mult, ALU.mult)

        def rms_norm(src_fn, dst_fn, ntok, l, gain_col, sqbuf, lnbuf, rsbuf, flag=False):
            rms_stats(src_fn, ntok, l, sqbuf, lnbuf, rsbuf, flag)
            rms_apply(src_fn, dst_fn, ntok, l, gain_col, rsbuf)

        def proj(b, ncol, wsl, rhs_fn):
            for kc in range(8):
                mm(b, wsl[:, kc, 0:ncol] if ncol != 128 else wsl[:, kc, :], rhs_fn(kc), kc == 0, kc == 7)

        class Pipe:
            NST = 5

            def __init__(self):
                self.items = []

            def _step(self, t):
                n = len(self.items)
                for st in range(self.NST - 1, -1, -1):
                    i = t - st
                    if 0 <= i < n and st in self.items[i]:
                        self.items[i][st]()

            def push(self, stages):
                self.items.append(stages)
                self._step(len(self.items) - 1)

            def flush(self):
                n = len(self.items)
                for t in range(n, n + self.NST - 1):
                    self._step(t)
                self.items = []

        item_ctr = [0]

        def qk_item(l, wsl, rhs_fn, ntok, gain_col, dst_view, fold, split_dst=None):
            k = item_ctr[0] % 2
            item_ctr[0] += 1
            sqk, lnk, rsk = sqs[k], lnvs[k], rstds[k]
            st = {}
            box = {}

            def s0():
                box["b"] = bank()
                proj(box["b"][:, 0:ntok], 128, wsl, rhs_fn)

            def s1():
                act(sqk[:, 0:ntok], box["b"][:, 0:ntok], AF.Square)

            def s2():
                box["b2"] = bank()
                mm(box["b2"][:, 0:ntok], bd_m, sqk[:, 0:ntok], True, True)

            def s3():
                act(lnk[:, 0:ntok], box["b2"][:, 0:ntok], AF.Ln, bias=spc(l, SP_EPS))
                act(rsk[:, 0:ntok], lnk[:, 0:ntok], AF.Exp, scale=-0.5)

            def s3b():
                psb = box["b"]
                if split_dst is None:
                    dve_stt(dst_view, fold(psb[:, 0:ntok]), spc(l, gain_col), fold(rsk[:, 0:ntok]), ALU.mult, ALU.mult)
                else:
                    for a in range(2):
                        pr = slice(64 * a, 64 * a + 64)
                        dve_stt(split_dst(a), fold(psb[pr, 0:ntok]), smallp[pr, l, gain_col:gain_col + 1],
                                fold(rsk[pr, 0:ntok]), ALU.mult, ALU.mult)

            def s34():
                s3()
                s3b()
            return {0: s0, 1: s1, 2: s2, 3: s34}

        def copy_item(wsl, rhs_fn, ntok, dst):
            box = {}

            def s0():
                box["b"] = bank()
                proj(box["b"][:, 0:ntok], 128, wsl, rhs_fn)

            def s1():
                dve_copy(dst, box["b"][:, 0:ntok])
            return {0: s0, 1: s1}

        def vblk_item(src_cols, dstv, nb):
            box = {}

            def s0():
                box["b"] = bank()
                for j, sc in enumerate(src_cols):
                    mm(box["b"][:, j * 128:(j + 1) * 128], sc, ident, True, True)

            def s1():
                srcv = box["b"][:, 0:nb * 128].rearrange("p (b h c) -> p b h c", b=nb, h=2)
                dve_copy(dstv, srcv)
            return {0: s0, 1: s1}

        def layer(l, last):
            pipe = Pipe()
            dma("pool", pm, pmix_d[l].rearrange("p (g n) -> p g n", g=4), writes=[pm])
            dve_memset(vlb[:, :, :, 64:128], 1.0)
            CH = 256 if l == 0 else 512
            NCH = LBT // CH

            def lb_stats(c):
                xb = xlbs[c % 2]
                for kc in range(8):
                    dma("sp", xb[:, kc, :], xlb_d[kc * 128:(kc + 1) * 128, c * CH:(c + 1) * CH], writes=[xb[:, kc, :]])
                rms_stats(lambda kc: xb[:, kc, :], CH, l, sqb, lnvA, rstdA, flag=True)

            def lb_apply(c):
                xb, hb_ = xlbs[c % 2], hlbs[c % 2]
                rms_apply(lambda kc: xb[:, kc, :], lambda kc: hb_[:, kc, :], CH, l, SP_NMIX, rstdA)
                if c == NCH - 1:
                    dve_copy(hT[:, :, 0:PRE], hb_[:, :, CH - PRE:CH])

            pending_apply = None
            pending_own = None
            for c8 in range(NCH):
                if l == 0:
                    hb = hlbs[c8 % 2]
                    if c8 == 0:
                        lb_stats(0)
                        lb_apply(0)
                    if c8 % 2 == 0:
                        tc = c8 // 2
                        if tc + 1 < NTC:
                            load_x(tc + 1)
                        rms_stats(lambda kc, tc=tc: xT[:, kc, tc * TCW:(tc + 1) * TCW], TCW, l, sqb, lnvA, rstdB)
                        pending_own = tc
                    if c8 + 1 < NCH:
                        lb_stats(c8 + 1)
                        pending_apply = c8 + 1
                else:
                    hb = hlbs512[c8 % 2]
                    srcx = xch_out[l - 1][c8].ap()
                    for kc in range(8):
                        dma("sp", hb[:, kc, :], srcx[kc * 128:(kc + 1) * 128, :], reads=[srcx], writes=[hb[:, kc, :]])
                    if c8 == LBT // CH - 1:
                        dve_ts(hT[:, :, 0:PRE], hb[:, :, CH - PRE:CH], spc(l, SP_FLAG), ALU.mult)
                for g in (2, 1, 0):
                    d = DIL[g]
                    lbg = 128 * d
                    lo = max(c8 * CH, LBT - lbg)
                    if lo >= (c8 + 1) * CH:
                        continue
                    n = (c8 + 1) * CH - lo
                    c0 = lo - c8 * CH
                    tpp = lo - (LBT - lbg)
                    kbase = (0, 128, 640)[g]
                    for p in range(2):
                        wk = load_win_cached(l, CI_K + 2 * g + p)
                        kv = ktlb[:, p, kbase:kbase + lbg].rearrange("p (r i) -> p r i", r=d)
                        i0 = tpp // d
                        pipe.push(qk_item(l, wk, lambda kc, c0=c0, n=n, hb=hb: hb[:, kc, c0:c0 + n], n, SP_KG,
                                          kv[:, :, i0:i0 + n // d], lambda a, d=d: a.rearrange("p (i r) -> p r i", r=d)))
                        wv = load_win_cached(l, CI_V + 2 * g + p)
                        pipe.push(copy_item(wv, lambda kc, c0=c0, n=n, hb=hb: hb[:, kc, c0:c0 + n], n,
                                            vtlb_view(g, p)[:, tpp:tpp + n]))
                if l == 0 and pending_own is not None:
                    tc = pending_own
                    rms_apply(lambda kc, tc=tc: xT[:, kc, tc * TCW:(tc + 1) * TCW],
                              lambda kc, tc=tc: hT[:, kc, PRE + tc * TCW:PRE + (tc + 1) * TCW], TCW, l, SP_NMIX, rstdB)
                    pending_own = None
                if l == 0 and pending_apply is not None:
                    lb_apply(pending_apply)
                    pending_apply = None
            pipe.flush()
            if 'B' not in phases:
                return
            for g in range(3):
                d = DIL[g]
                for p in range(2):
                    src = vtlb_view(g, p)
                    for r0 in range(0, d, 4):
                        nb = min(4, d - r0)
                        cols = [src[:, (r0 + j):(r0 + j) + 127 * d + 1:d] if d > 1 else src[:, 0:128] for j in range(nb)]
                        blk0 = LBBLK[g] + r0
                        dstv = vlb[:, p, blk0:blk0 + nb, :].rearrange("p b (h c) -> p b h c", h=3)[:, :, 0:3:2, :]
                        pipe.push(vblk_item(cols, dstv, nb))
            pipe.flush()

            if 'C' not in phases:
                return
            dve_memset(vown[:, :, 64:128], 1.0)
            dve_memset(qtf2[64:128, 0, :], 0.0)
            dve_memset(qtf2[0:64, 1, :], 0.0)
            for p in range(2):
                for g in range(3):
                    d = DIL[g]
                    Lg = T // d
                    NB = Lg // 128
                    kbase = (0, 128, 640)[g]
                    fold = (lambda a, d=d: a.rearrange("p (i r) -> p r i", r=d))
                    hsl = lambda kc, tc: hT[:, kc, PRE + tc * TCW:PRE + (tc + 1) * TCW]
                    wk = load_win(l, CI_K + 2 * g + p)
                    ktv = ktown.rearrange("p (r i) -> p r i", r=d)
                    for tc in range(NTC):
                        i0 = tc * TCW // d
                        pipe.push(qk_item(l, wk, lambda kc, tc=tc: hsl(kc, tc), TCW, SP_KG,
                                          ktv[:, :, i0:i0 + TCW // d], fold))
                    wv = load_win(l, CI_V + 2 * g + p)
                    for tc in range(NTC):
                        pipe.push(copy_item(wv, lambda kc, tc=tc: hsl(kc, tc), TCW, vt[:, tc * TCW:(tc + 1) * TCW]))
                    wq = load_win(l, CI_Q + 2 * g + p)
                    for tc in range(NTC):
                        i0 = tc * TCW // d
                        pipe.push(qk_item(l, wq, lambda kc, tc=tc: hsl(kc, tc), TCW, SP_QG, None, fold,
                                          split_dst=lambda a, i0=i0, d=d: qtf2[64 * a:64 * a + 64, a, :].rearrange(
                                              "p (r i) -> p r i", r=d)[:, :, i0:i0 + TCW // d]))
                    for q4 in range(4):
                        cols = []
                        for j in range(4):
                            blk = q4 * 4 + j
                            r, n = blk // NB, blk % NB
                            s0_ = r + 128 * d * n
                            cols.append(vt[:, s0_:s0_ + 127 * d + 1:d] if d > 1 else vt[:, s0_:s0_ + 128])
                        dstv = vown[:, q4 * 4:q4 * 4 + 4, :].rearrange("p b (h c) -> p b h c", h=3)[:, :, 0:3:2, :]
                        pipe.push(vblk_item(cols, dstv, 4))
                    for bp in range(8):
                        pipe.push(attn_item(l, p, g, bp, d, NB, kbase))
                for tc in range(NTC):
                    pipe.push({3: (lambda tc=tc, p=p: norm_attn_act(tc, p)), 4: (lambda tc=tc, p=p: norm_attn(tc, p))})
            pipe.flush()

        def norm_attn_act(tc, p):
            ts = slice(tc * TCW, (tc + 1) * TCW)
            rt = rtmps[tc % 2]
            act(rt[0:64, :], acc[64:128, 0, ts], AF.Ln)
            act(rt[0:64, :], rt[0:64, :], AF.Exp, scale=-1.0)
            act(rt[64:128, :], acc[0:64, 1, ts], AF.Ln)
            act(rt[64:128, :], rt[64:128, :], AF.Exp, scale=-1.0)

        def norm_attn(tc, p):
            ts = slice(tc * TCW, (tc + 1) * TCW)
            rt = rtmps[tc % 2]
            dve_tt(yattn[0:64, p, ts], acc[0:64, 0, ts], rt[0:64, :], ALU.mult)
            dve_tt(yattn[64:128, p, ts], acc[64:128, 1, ts], rt[64:128, :], ALU.mult)

        def attn_item(l, p, g, bp, d, NB, kbase):
            blks = (2 * bp, 2 * bp + 1)
            box = {}

            def s0():
                box["s"] = []
                for j, blk in enumerate(blks):
                    r, n = blk // NB, blk % NB
                    sb_ = bank()
                    mm(sb_, ident, mask_f if n == 0 else mask_n, True, False, skip=True)
                    for a in range(2):
                        qv = qtf2[:, a, blk * 128:(blk + 1) * 128]
                        if n == 0:
                            kprev = ktlb[:, p, kbase + r * 128:kbase + (r + 1) * 128]
                        else:
                            kprev = ktown[:, (blk - 1) * 128:blk * 128]
                        kcur = ktown[:, blk * 128:(blk + 1) * 128]
                        mm(sb_[:, 256 * a:256 * a + 128], kprev, qv, False, False, skip=True)
                        mm(sb_[:, 256 * a + 128:256 * a + 256], kcur, qv, False, a == 1, skip=True)
                    box["s"].append(sb_)

            def s1():
                box["pt"] = []
                for j, blk in enumerate(blks):
                    pt = pts[(2 * bp + j) % 4]
                    act(pt, box["s"][j], AF.Exp, scale=0.125)
                    box["pt"].append(pt)

            def s2():
                ob = bank()
                box["o"] = ob
                for a in range(2):
                    for j, blk in enumerate(blks):
                        r, n = blk // NB, blk % NB
                        pt = box["pt"][j]
                        if n == 0:
                            vprev = vlb[:, p, LBBLK[g] + r, 64 * a:64 * a + 128]
                        else:
                            vprev = vown[:, blk - 1, 64 * a:64 * a + 128]
                        vcur = vown[:, blk, 64 * a:64 * a + 128]
                        oslot = ob[:, (2 * a + j) * 128:(2 * a + j + 1) * 128]
                        mm(oslot, vprev, pt[:, 256 * a:256 * a + 128], True, False, skip=True)
                        mm(oslot, vcur, pt[:, 256 * a + 128:256 * a + 256], False, True, skip=True)

            def s3():
                ob = box["o"]
                for a in range(2):
                    src = ob[:, 256 * a:256 * a + 256]
                    r, n = blks[0] // NB, blks[0] % NB
                    if g < 2:
                        s0_ = r + 128 * d * n
                        av = acc[:, a, s0_:s0_ + 255 * d + 1:d] if d > 1 else acc[:, a, s0_:s0_ + 256]
                    else:
                        av = acc[:, a, :].rearrange("p (i r) -> p r i", r=16)[:, r:r + 2, :]
                        src = src.rearrange("p (q i) -> p q i", q=2)
                    if g == 0:
                        dve_copy(av, src)
                    else:
                        dve_tt(av, av, src, ALU.add)
            return {0: s0, 1: s1, 2: s2, 3: s3}

        def pool_item(l, g, w, wp, tc):
            k = item_ctr[0] % 2
            item_ctr[0] += 1
            c0 = PRE + tc * TCW
            c1 = c0 + TCW
            LW = TCW + 16
            box = {}

            def s0():
                box["b"] = bank()
                proj(box["b"], 128, wp, lambda kc: hT[:, kc, c0:c1])
                if tc == 0:
                    box["bp"] = bank()
                    proj(box["bp"][:, 0:PRE], 128, wp, lambda kc: hT[:, kc, 0:PRE])

            def s1():
                if tc == 0:
                    act(Ubuf[:, 0:PRE], box["bp"][:, 0:PRE], AF.Copy)
                act(Ubuf[:, c0:c1], box["b"], AF.Copy)

            def s2():
                bufs = [SAl[k], SBl[k]]
                cur, cur_is_u = Ubuf, True
                sh, j = 1, 0
                while sh < w:
                    lo = 16 - (w - 2 * sh)
                    nxt = bufs[j % 2]
                    if cur_is_u:
                        u0 = c0 - 16 + lo
                        dve_tt(nxt[:, lo:LW], Ubuf[:, u0:c1], Ubuf[:, u0 - sh:c1 - sh], ALU.add)
                    else:
                        dve_tt(nxt[:, lo:LW], cur[:, lo:LW], cur[:, lo - sh:LW - sh], ALU.add)
                    cur, cur_is_u = nxt, False
                    sh *= 2
                    j += 1
                dch = Dch[k]
                dve_stt(dch[:, :], cur[:, 16:LW], 1.0 / w, Ubuf[:, c0:c1], ALU.mult, ALU.subtract)
                if tc == 0:
                    dve_tt(dtmp[:, :], cur[:, 16:32], invc[:, g, :], ALU.mult)
                    dve_tt(dch[:, 0:16], dtmp[:, :], Ubuf[:, PRE:PRE + 16], ALU.subtract)

            def s3():
                box["b2"] = bank()
                mm(box["b2"], pm[:, g, :], Dch[k][:, :], True, True)

            def s4():
                act(ypool[:, g, tc * TCW:(tc + 1) * TCW], box["b2"], AF.Identity, scale=spc(l, SP_PSC + g))
            return {0: s0, 1: s1, 2: s2, 3: s3, 4: s4}

        def conv_item(l, c, wc, wx, wb, tc):
            k = item_ctr[0] % 2
            item_ctr[0] += 1
            c0 = PRE + tc * TCW
            c1 = c0 + TCW
            LW = TCW + 16
            box = {}

            def s0():
                box["b1"] = bank()
                proj(box["b1"], 128, wc, lambda kc: hT[:, kc, c0:c1])
                box["b2"] = bank()
                proj(box["b2"], 128, wx, lambda kc: hT[:, kc, c0:c1])
                if tc == 0:
                    box["bp"] = bank()
                    proj(box["bp"][:, 0:PRE], 128, wc, lambda kc: hT[:, kc, 0:PRE])
                    proj(box["bp"][:, 256:256 + PRE], 128, wx, lambda kc: hT[:, kc, 0:PRE])

            def s1():
                if tc == 0:
                    act(CSl[k][:, 0:PRE], box["bp"][:, 0:PRE], AF.Copy)
                act(CSl[k][:, 16:LW], box["b1"], AF.Copy)

            def s2():
                if tc == 0:
                    dve_tt(UC[:, 0:PRE], CSl[k][:, 0:PRE], box["bp"][:, 256:256 + PRE], ALU.mult)
                dve_tt(UC[:, c0:c1], CSl[k][:, 16:LW], box["b2"], ALU.mult)
                y = Ych[k]
                dve_ts(y[:, :], UC[:, c0:c1], spc(l, SP_CW + 2 * 4 + c), ALU.mult)
                dve_stt(y[:, :], UC[:, c0 - 1:c1 - 1], spc(l, SP_CW + 1 * 4 + c), y[:, :], ALU.mult, ALU.add)
                dve_stt(y[:, :], UC[:, c0 - 2:c1 - 2], spc(l, SP_CW + 0 * 4 + c), y[:, :], ALU.mult, ALU.add)

            def s3():
                box["b3"] = bank()
                proj(box["b3"], 128, wb, lambda kc: hT[:, kc, c0:c1])

            def s4():
                dve_tt(yconv[:, c, tc * TCW:(tc + 1) * TCW], Ych[k][:, :], box["b3"], ALU.mult)
            return {0: s0, 1: s1, 2: s2, 3: s3, 4: s4}

        def layer_rest(l, last):
            if 'D' not in phases:
                return
            pipe = Pipe()
            for g in range(4):
                wp = load_win(l, CI_POOL + g)
                for tc in range(NTC):
                    pipe.push(pool_item(l, g, POOLW[g], wp, tc))
            pipe.flush()

            if 'E' not in phases:
                return
            for c in range(4):
                wc = load_win(l, CI_CC + c)
                wx = load_win(l, CI_CX + c)
                wb = load_win(l, CI_CB + c)
                for tc in range(NTC):
                    pipe.push(conv_item(l, c, wc, wx, wb, tc))
            pipe.flush()

            if debug and l == 0:
                dbg_dump("d_yattn", yattn, 2)
                dbg_dump("d_ypool", ypool, 4)
                dbg_dump("d_yconv", yconv, 4)
                dbg_dump("d_h", hT[:, :, PRE:PRE + T], 8)

            if 'F' not in phases:
                return
            for fc in range(8):
                us = upsl[fc % 2]
                dma("pool", us, ups_d[l, fc].rearrange("p (k n) -> p k n", k=10), writes=[us])
                wg = [load_win(l, CI_G + br * 8 + fc) for br in range(3)]
                for tc in range(NTC):
                    ts = slice(tc * TCW, (tc + 1) * TCW)
                    hs = slice(PRE + tc * TCW, PRE + (tc + 1) * TCW)
                    gb = []
                    for br in range(3):
                        b = bank()
                        proj(b, 128, wg[br], lambda kc, hs=hs: hT[:, kc, hs])
                        gb.append(b)
                    ub = []
                    for br, (src, nk, k0) in enumerate(((ypool, 4, 0), (yconv, 4, 4), (yattn, 2, 8))):
                        b = bank()
                        for k in range(nk):
                            mm(b, us[:, k0 + k, :], src[:, k, ts], k == 0, k == nk - 1)
                        ub.append(b)
                    for br in range(3):
                        act(sg[:, br, :], gb[br], AF.Sigmoid, bias=spc(l, SP_BG + br * 8 + fc))
                    dve_tt(mt[:, :], sg[:, 0, :], ub[0], ALU.mult)
                    dve_tt(mt2[:, :], sg[:, 1, :], ub[1], ALU.mult)
                    dve_tt(mt[:, :], mt[:, :], mt2[:, :], ALU.add)
                    dve_tt(mt2[:, :], sg[:, 2, :], ub[2], ALU.mult)
                    dve_tt(merged[:, fc, ts], mt[:, :], mt2[:, :], ALU.add)
            if debug and l == 0:
                dbg_dump("d_merged", merged, 8)
            for fo in range(8):
                wo = load_wchunk(wo_d[l, fo])
                for tc in range(NTC):
                    ts = slice(tc * TCW, (tc + 1) * TCW)
                    b = bank()
                    for kc in range(8):
                        mm(b, wo[:, kc, :], merged[:, kc, ts], kc == 0, kc == 7)
                    dve_tt(xT[:, fo, ts], xT[:, fo, ts], b, ALU.add)
            if debug and l == 0:
                dbg_dump_f32("d_x1", xT, 8)

            if 'G' not in phases:
                return
            def ffn_norm(tc):
                rms_norm(lambda kc: xT[:, kc, tc * TCW:(tc + 1) * TCW],
                         lambda kc: hT[:, kc, PRE + tc * TCW:PRE + (tc + 1) * TCW],
                         TCW, l, SP_NMLP, sqb2, lnv2, rstd2)
            ffn_norm(0)
            for q in range(4):
                if q == 3 and not last:
                    prefetch_win(l + 1, [CI_K + 4, CI_V + 4, CI_K + 5, CI_V + 5])
                w1 = w1s[q % 2]
                w2 = w2s[q % 2]
                for j in range(4):
                    dma("pool", w1[:, :, j * 256:(j + 1) * 256],
                        w1_d[l, q].rearrange("p (k n) -> p k n", k=8)[:, :, j * 256:(j + 1) * 256],
                        writes=[w1[:, :, j * 256:(j + 1) * 256]])
                for j in range(4):
                    dma("pool", w2[:, :, j * 256:(j + 1) * 256],
                        w2_d[l, q].rearrange("p (k n) -> p k n", k=8)[:, :, j * 256:(j + 1) * 256],
                        writes=[w2[:, :, j * 256:(j + 1) * 256]])
                for tc in range(NTC):
                    if q == 0 and tc + 1 < NTC:
                        ffn_norm(tc + 1)
                    ts = slice(tc * TCW, (tc + 1) * TCW)
                    hs = slice(PRE + tc * TCW, PRE + (tc + 1) * TCW)
                    fT = fTs[tc % 2]
                    for ffc in range(8):
                        b = bank()
                        for kc in range(8):
                            mm(b, w1[:, kc, ffc * 128:(ffc + 1) * 128], hT[:, kc, hs], kc == 0, kc == 7)
                        rl = relus[ffc % 2]
                        act(rl[:, :], b, AF.Relu)
                        dve_tt(fT[:, ffc, :], rl[:, :], b, ALU.mult)
                    for fo in range(8):
                        b = bank()
                        for ffc in range(8):
                            mm(b, w2[:, ffc, fo * 128:(fo + 1) * 128], fT[:, ffc, :], ffc == 0, ffc == 7)
                        dve_tt(xT[:, fo, ts], xT[:, fo, ts], b, ALU.add)
                    if q == 3 and not last:
                        rms_norm(lambda kc: xT[:, kc, ts], lambda kc: hT[:, kc, hs],
                                 TCW, l + 1, SP_NMIX, sqb2, lnv2, rstd2)
                        xi = xch_in[l][tc].ap()
                        xo = xch_out[l][tc].ap()
                        for kc in range(8):
                            dma("sp", xi[kc * 128:(kc + 1) * 128, :], hT[:, kc, hs], reads=[hT[:, kc, hs]], writes=[xi[kc * 128:(kc + 1) * 128, :]])
                        R.add("pool", lambda e, xi=xi, xo=xo: e.collective_compute(
                            "AllGather", ALU.bypass, replica_groups=[[0, 1], [2, 3], [4, 5], [6, 7]],
                            ins=[xi.opt()], outs=[xo.opt()]), reads=[xi], writes=[xo], cc=True)
                    if q == 3 and last:
                        for kc in range(8):
                            dma("sp", out_d[kc * 128:(kc + 1) * 128, ts], xT[:, kc, ts], reads=[xT[:, kc, ts]])

        vtlb_off = {}
        o_tmp = o_vtlb_small
        for g in range(2):
            for p in range(2):
                vtlb_off[(g, p)] = o_tmp
                o_tmp += 128 * DIL[g] * 2
        for p in range(2):
            vtlb_off[(2, p)] = o_yattn + p * 4096

        def vtlb_view(g, p):
            return A.view(vtlb_off[(g, p)], [128 * DIL[g]], BF16)

        dbg_stage = A.view(o_dbg, [T], F32) if debug else None

        def dbg_dump(name, src3, nch):
            for c in range(nch):
                act(dbg_stage[:, :], src3[:, c, :], AF.Copy)
                dma("sp", dbg[name][c * 128:(c + 1) * 128, :], dbg_stage[:, :], reads=[dbg_stage[:, :]])

        def dbg_dump_f32(name, src3, nch):
            for c in range(nch):
                dma("sp", dbg[name][c * 128:(c + 1) * 128, :], src3[:, c, :], reads=[src3[:, c, :]])

        for l in range(L):
            layer(l, l == L - 1)
            layer_rest(l, l == L - 1)

        R.emit(nc, sems, dma_sems)
    return nc


def _prep_weights(layers, w_in, pool_mix, w_pool_up, w_conv_out, w_attn_up, w_o, w_ff1, w_ff2):
    nl = len(layers)
    win = np.empty((nl, NWC, 128, NKC * 128), np.float32)
    pmix = np.empty((nl, 128, 4 * 128), np.float32)
    ups = np.empty((nl, 8, 128, 10 * 128), np.float32)
    wo = np.empty((nl, 8, 128, NKC * 128), np.float32)
    w1 = np.empty((nl, 4, 128, NKC * 1024), np.float32)
    w2 = np.empty((nl, 4, 128, 8 * 1024), np.float32)
    for i, l in enumerate(layers):
        win[i] = w_in[l].reshape(NKC, 128, NWC, 128).transpose(2, 1, 0, 3).reshape(NWC, 128, NKC * 128)
        pmix[i] = pool_mix[l].transpose(1, 0, 2).reshape(128, 512)
        cat = np.concatenate([w_pool_up[l], w_conv_out[l], w_attn_up[l]], axis=0)
        ups[i] = cat.reshape(10, 128, 8, 128).transpose(2, 1, 0, 3).reshape(8, 128, 1280)
        wo[i] = w_o[l].reshape(NKC, 128, 8, 128).transpose(2, 1, 0, 3).reshape(8, 128, NKC * 128)
        w1[i] = w_ff1[l].reshape(NKC, 128, 4, 1024).transpose(2, 1, 0, 3).reshape(4, 128, NKC * 1024)
        w2[i] = w_ff2[l].reshape(4, 8, 128, 1024).transpose(0, 2, 1, 3).reshape(4, 128, 8 * 1024)
    return dict(win=win, pmix=pmix, ups=ups, wo=wo, w1=w1, w2=w2)


def _prep_small(layers, is_second_half, norm_mix, b_gate, pool_scale, conv_w, q_gain, k_gain, norm_mlp):
    nl = len(layers)
    sp = np.zeros((nl, 128, NSP), np.float32)
    for i, l in enumerate(layers):
        sp[i, :, SP_NMIX:SP_NMIX + 8] = norm_mix[l].reshape(8, 128).T
        sp[i, :, SP_NMLP:SP_NMLP + 8] = norm_mlp[l].reshape(8, 128).T
        sp[i, :, SP_BG:SP_BG + 24] = b_gate[l].reshape(24, 128).T
        sp[i, :, SP_PSC:SP_PSC + 4] = pool_scale[l].reshape(4, 128).T
        sp[i, :, SP_CW:SP_CW + 12] = conv_w[l].reshape(3, 4, 128).transpose(2, 0, 1).reshape(128, 12)
        sp[i, :, SP_QG] = np.tile(q_gain[l], 2)
        sp[i, :, SP_KG] = np.tile(k_gain[l], 2)
        sp[i, :, SP_FLAG] = 1.0 if is_second_half else 0.0
        sp[i, :, SP_EPS] = EPS
    return sp


def _consts(is_second_half):
    cst = np.zeros((128, NCST), np.float32)
    cst[:, C_ID:C_ID + 128] = np.eye(128, dtype=np.float32)
    cst[:, C_ONES:C_ONES + 128] = 1.0 / 1024.0
    bd = np.zeros((128, 128), np.float32)
    bd[0:64, 0:64] = 1.0 / 64.0
    bd[64:128, 64:128] = 1.0 / 64.0
    cst[:, C_BD:C_BD + 128] = bd
    k = np.arange(128)[:, None]
    q = np.arange(128)[None, :]
    mprev = np.where(k >= q, 0.0, NEG).astype(np.float32)
    mcur = np.where(k <= q, 0.0, NEG).astype(np.float32)
    mn = np.concatenate([mprev, mcur], axis=1)
    mfirst = mn if is_second_half else np.concatenate([np.full((128, 128), NEG, np.float32), mcur], axis=1)
    cst[:, C_MN:C_MN + 512] = np.concatenate([mn, mn], axis=1)
    cst[:, C_MF:C_MF + 512] = np.concatenate([mfirst, mfirst], axis=1)
    invc = np.zeros((128, 64), np.float32)
    for g, w in enumerate(POOLW):
        for j in range(16):
            cnt = w if is_second_half else min(j + 1, w)
            invc[:, g * 16 + j] = 1.0 / cnt
    return cst, invc


_NC_CACHE = {}


def _get_nc(n_layers, debug=False):
    key = (n_layers, debug)
    if key not in _NC_CACHE:
        _NC_CACHE[key] = build_program(n_layers, debug)
    return _NC_CACHE[key]


def _run_layers(x_cur, layers, params, debug=False):
    B = x_cur.shape[0]
    nc = _get_nc(len(layers), debug)
    wts = _prep_weights(layers, params["w_in"], params["pool_mix"], params["w_pool_up"], params["w_conv_out"],
                        params["w_attn_up"], params["w_o"], params["w_ff1"], params["w_ff2"])
    in_maps = []
    for core in range(N_CORES):
        b, half = core // 2, core % 2
        xo = np.ascontiguousarray(x_cur[b, half * T:(half + 1) * T, :].T)
        if half == 1:
            xl = np.ascontiguousarray(x_cur[b, 0:T, :].T)
        else:
            xl = np.zeros((D, LBT), np.float32)
        cst, invc = _consts(half == 1)
        sp = _prep_small(layers, half == 1, params["norm_mix"], params["b_gate"], params["pool_scale"],
                         params["conv_w"], params["q_gain"], params["k_gain"], params["norm_mlp"])
        m = dict(xT=xo, xlbT=xl, smallp=sp, cst=cst, invc=invc)
        m.update(wts)
        in_maps.append(m)
    res = run_bass_kernel_spmd(nc, in_maps, core_ids=list(range(N_CORES)))
    out = np.empty_like(x_cur)
    for core in range(N_CORES):
        b, half = core // 2, core % 2
        out[b, half * T:(half + 1) * T, :] = res.results[core]["outT"].T
    return out, res


def kernel(x, norm_mix, w_in, b_gate, pool_mix, pool_scale, conv_w, q_gain, k_gain,
           w_pool_up, w_conv_out, w_attn_up, w_o, norm_mlp, w_ff1, w_ff2):
    params = dict(norm_mix=norm_mix, w_in=w_in, b_gate=b_gate, pool_mix=pool_mix, pool_scale=pool_scale,
                  conv_w=conv_w, q_gain=q_gain, k_gain=k_gain, w_pool_up=w_pool_up, w_conv_out=w_conv_out,
                  w_attn_up=w_attn_up, w_o=w_o, norm_mlp=norm_mlp, w_ff1=w_ff1, w_ff2=w_ff2)
    params = {k: np.asarray(v, dtype=np.float32) for k, v in params.items()}
    cur = np.asarray(x, dtype=np.float32)
    out, _ = _run_layers(cur, list(range(DEPTH)), params)
    return out
```

```python
import numpy as np
import concourse.bass as bass
import concourse.mybir as mybir
from concourse.bass_utils import run_bass_kernel_spmd

F32 = mybir.dt.float32
BF16 = mybir.dt.bfloat16
U8 = mybir.dt.uint8
ALU = mybir.AluOpType
AF = mybir.ActivationFunctionType
ESZ = {F32: 4, BF16: 2, U8: 1}

D = 1024
T = 2048
LBT = 2048
PRE = 16
NKC = 8
TCW = 512
NTC = 4
DIL = (1, 4, 16)
POOLW = (2, 4, 8, 16)
EPS = 1e-6
NEG = -30000.0
N_CORES = 8
DEPTH = 2
IN_COLS = 7424
NWC = 58
CI_POOL, CI_CB, CI_CC, CI_CX, CI_Q, CI_K, CI_V, CI_G = 0, 4, 8, 12, 16, 22, 28, 34
SP_NMIX, SP_NMLP, SP_BG, SP_PSC, SP_CW, SP_QG, SP_KG, SP_FLAG, SP_EPS = 0, 8, 16, 40, 44, 56, 57, 58, 59
NSP = 64
C_ID, C_ONES, C_BD, C_MN, C_MF = 0, 128, 256, 384, 896
NCST = 1408
LBBLK = (0, 1, 5)
NLBBLK = 21


def _region(ap):
    t = ap.tensor
    dims = list(ap.ap)
    pstep = dims[0][0]
    esz = ESZ[ap.dtype]
    off = ap.offset
    f0 = off % pstep if pstep > 0 else off
    ext = 1
    for s, c in dims[1:]:
        ext += (c - 1) * abs(s)
    if t.name.startswith("ps"):
        return (t.name, 0, 2048)
    if t.name.startswith("xch"):
        ext = 1
        for s_, c in dims:
            ext += (c - 1) * abs(s_)
        return (t.name, ap.offset, ap.offset + ext)
    return (t.name, f0 * esz, (f0 + ext) * esz)


class Rec:
    ENGS = ("pe", "act", "dve", "pool", "sp")

    def __init__(self):
        self.ops = []
        self.recs = {}

    def _access(self, reg, idx, eng, is_write, deps):
        name, lo, hi = reg
        lst = self.recs.setdefault(name, [])
        if is_write:
            keep = []
            for r in lst:
                if r[0] < hi and lo < r[1]:
                    deps.add(r[2])
                    if lo <= r[0] and r[1] <= hi:
                        continue
                keep.append(r)
            keep.append((lo, hi, idx, eng, True))
            self.recs[name] = keep
        else:
            keep = []
            for r in lst:
                if r[4]:
                    if r[0] < hi and lo < r[1]:
                        deps.add(r[2])
                    keep.append(r)
                else:
                    if r[3] == eng and eng not in ("sp", "pool") and lo <= r[0] and r[1] <= hi:
                        continue
                    keep.append(r)
            keep.append((lo, hi, idx, eng, False))
            self.recs[name] = keep

    def add(self, eng, fn, reads=(), writes=(), dma=False, cc=False):
        idx = len(self.ops)
        deps = set()
        for ap in reads:
            if ap is None:
                continue
            self._access(_region(ap), idx, eng, False, deps)
        for ap in writes:
            self._access(_region(ap), idx, eng, True, deps)
        deps.discard(idx)
        self.ops.append({"eng": eng, "fn": fn, "deps": deps, "dma": dma or cc, "inc": 1 if cc else 16,
                         "semkey": "cc" if cc else eng})
        return idx

    def emit(self, nc, sems, dma_sems, final_wait_eng="sp"):
        ops = self.ops
        signal = [False] * len(ops)
        for i, o in enumerate(ops):
            if o["dma"]:
                signal[i] = True
            for p in o["deps"]:
                if ops[p]["eng"] == "pe" and o["eng"] == "pe" and not ops[p]["dma"] and not o["dma"]:
                    continue
                signal[p] = True
        cnt = {e: 0 for e in self.ENGS}
        dcnt = {e: 0 for e in self.ENGS}
        duse = {}
        for i, o in enumerate(ops):
            if o["dma"]:
                pool = dma_sems[o["semkey"]]
                s = pool[dcnt.get(o["semkey"], 0) % len(pool)]
                dcnt[o["semkey"]] = dcnt.get(o["semkey"], 0) + 1
                duse[s] = duse.get(s, 0) + o["inc"]
                o["sem"] = s
                o["val"] = duse[s]
            elif signal[i]:
                cnt[o["eng"]] += 1
                o["sem"] = sems[o["eng"]]
                o["val"] = cnt[o["eng"]]
        by_eng = {e: [] for e in self.ENGS}
        for i, o in enumerate(ops):
            by_eng[o["eng"]].append(i)
        final_dma = [o for o in ops if o["dma"]]

        def run(eng_name, eng):
            waited = {}
            for i in by_eng[eng_name]:
                o = ops[i]
                w = {}
                for p in o["deps"]:
                    po = ops[p]
                    if po["eng"] == "pe" and eng_name == "pe" and not po["dma"] and not o["dma"]:
                        continue
                    s = po["sem"]
                    if w.get(s, (None, 0))[1] < po["val"]:
                        w[s] = (s, po["val"])
                if o["dma"] and o["val"] > o["inc"]:
                    s = o["sem"]
                    if w.get(s, (None, 0))[1] < o["val"] - o["inc"]:
                        w[s] = (s, o["val"] - o["inc"])
                for s, v in w.values():
                    if waited.get(s, 0) < v:
                        eng.wait_ge(s, v)
                        waited[s] = v
                inst = o["fn"](eng)
                if o["dma"]:
                    inst.then_inc(o["sem"], o["inc"])
                elif signal[i]:
                    inst.then_inc(o["sem"], 1)
            if eng_name == final_wait_eng:
                last = {}
                for o in final_dma:
                    if last.get(o["sem"], (None, 0))[1] < o["val"]:
                        last[o["sem"]] = (o["sem"], o["val"])
                for s_, v in last.values():
                    if waited.get(s_, 0) < v:
                        eng.wait_ge(s_, v)
                        waited[s_] = v

        with nc.Block() as block:
            @block.tensor
            def _(e):
                run("pe", e)

            @block.scalar
            def _(e):
                run("act", e)

            @block.vector
            def _(e):
                run("dve", e)

            @block.gpsimd
            def _(e):
                run("pool", e)

            @block.sync
            def _(e):
                run("sp", e)


class Arena:
    def __init__(self, tensor, size):
        self.t = tensor
        self.size = size

    def view(self, off, shape, dt):
        n = 1
        for s in shape:
            n *= s
        nb = n * ESZ[dt]
        assert off % 4 == 0 and off + nb <= self.size, (off, nb, self.size)
        v = self.t[:, off:off + nb].bitcast(dt)
        if len(shape) == 2:
            return v.rearrange("p (a b) -> p a b", a=shape[0])
        if len(shape) == 3:
            return v.rearrange("p (a b c) -> p a b c", a=shape[0], b=shape[1])
        return v


def build_program(n_layers, debug=False, phases="ABCDEFGkvqabcn"):
    nc = bass.Bass("TRN2", target_bir_lowering=False)
    L = n_layers
    dr = {}

    def din(name, shape):
        dr[name] = nc.dram_tensor(name, shape, F32, kind="ExternalInput").ap()
        return dr[name]

    xT_d = din("xT", [D, T])
    xlb_d = din("xlbT", [LBT // 256, 128, NKC * 256])
    win_d = din("win", [L, NWC, 128, NKC * 128])
    pmix_d = din("pmix", [L, 128, 4 * 128])
    ups_d = din("ups", [L, 8, 128, 10 * 128])
    wo_d = din("wo", [L, 8, 128, NKC * 128])
    w1_d = din("w1", [L, 4, 128, NKC * 1024])
    w2_d = din("w2", [L, 4, 128, 8 * 1024])
    smallp_d = din("smallp", [L, 128, NSP])
    cst_d = din("cst", [128, NCST])
    invc_d = din("invc", [128, 64])
    out_d = nc.dram_tensor("outT", [D, T], F32, kind="ExternalOutput").ap()
    dbg = {}
    if debug:
        for nm, rows in (("d_yattn", 256), ("d_ypool", 512), ("d_yconv", 512), ("d_merged", 1024), ("d_h", 1024), ("d_x1", 1024)):
            dbg[nm] = nc.dram_tensor(nm, [rows, T], F32, kind="ExternalOutput").ap()

    ARENA = 207 * 1024
    R = Rec()
    import contextlib
    with contextlib.ExitStack() as es:
        arena_t = es.enter_context(nc.sbuf_tensor("arena", [128, ARENA], U8))
        A = Arena(arena_t, ARENA)
        ps = [es.enter_context(nc.psum_tensor(f"ps{i}", [128, 512], F32)) for i in range(8)]
        sems = {e: es.enter_context(nc.semaphore(f"s_{e}")) for e in ("pe", "act", "dve", "pool", "sp")}
        dma_sems = {
            "pool": [es.enter_context(nc.semaphore(f"dp{i}")) for i in range(12)],
            "sp": [es.enter_context(nc.semaphore(f"ds{i}")) for i in range(8)],
            "cc": [es.enter_context(nc.semaphore(f"dc{i}")) for i in range(4)],
        }
        xch_in = [[nc.dram_tensor(f"xch_in_{l}_{c}", [D, 512], BF16) for c in range(4)] for l in range(L - 1)]
        xch_out = [[nc.dram_tensor(f"xch_out_{l}_{c}", [2 * D, 512], BF16) for c in range(4)] for l in range(L - 1)]

        off = 0

        def alloc(nbytes):
            nonlocal off
            o = off
            off += (nbytes + 63) // 64 * 64
            return o

        o_xT = alloc(8 * T * 4)
        o_hT = alloc(8 * (PRE + T) * 2)
        o_cst = alloc(NCST * 2)
        o_invc = alloc(64 * 4)
        o_smallp = alloc(L * NSP * 4)
        NSLOT = 5
        o_wslot = alloc(NSLOT * 8 * 128 * 2)
        o_pm = alloc(4 * 128 * 2)
        o_yattn = alloc(2 * T * 2)
        o_ph = off
        o_ktlb = alloc(2 * 2688 * 2)
        o_vlb = alloc(2 * NLBBLK * 192 * 2)
        o_vt = alloc(T * 2)
        o_sq = alloc(2 * 512 * 2)
        o_lnv = alloc(2 * 512 * 4)
        o_rstd = alloc(2 * 512 * 4)
        o_alias = off
        o_acc = alloc(2 * T * 4)
        o_ktown = alloc(T * 2)
        o_vown = alloc(16 * 192 * 2)
        o_qtf = alloc(2 * T * 2)
        o_pt = alloc(4 * 512 * 2)
        o_rtmp = alloc(2 * 512 * 4)
        end_c = off
        off = o_alias
        o_xlb = alloc(2 * 8 * 256 * 4)
        o_hlb = alloc(2 * 8 * 256 * 2)
        o_sqb = alloc(8 * 512 * 2)
        assert off <= end_c
        off = end_c
        o_lnvA = alloc(512 * 4)
        o_rstdA = alloc(512 * 4)
        assert off <= ARENA, off
        o_vtlb_small = o_alias + 32768
        assert o_vtlb_small + 2560 <= end_c and o_sqb + 8 * 512 * 2 <= o_vtlb_small
        off = o_ph
        o_ypool = alloc(4 * T * 2)
        o_yconv = alloc(4 * T * 2)
        o_mix = off
        o_U = alloc((PRE + T) * 4)
        o_SAl = alloc(2 * 528 * 4)
        o_SBl = alloc(2 * 528 * 4)
        o_Dch = alloc(2 * 512 * 2)
        o_dt = alloc(16 * 4)
        assert off <= ARENA
        off = o_mix
        o_UC = alloc((PRE + T) * 4)
        o_CSl = alloc(2 * 528 * 4)
        o_Ych = alloc(2 * 512 * 4)
        assert off <= ARENA
        off = o_mix
        o_merged = alloc(8 * T * 2)
        o_upsl = alloc(2 * 10 * 128 * 2)
        o_sg = alloc(3 * 512 * 4)
        o_mt = alloc(512 * 4)
        o_mt2 = alloc(512 * 4)
        assert off <= ARENA, off
        o_dbg = ARENA - T * 4
        if debug:
            assert off <= o_dbg, (off, o_dbg)
        off = o_yattn
        o_w1 = alloc(2 * 8 * 1024 * 2)
        o_w2 = alloc(2 * 8 * 1024 * 2)
        o_fT = alloc(2 * 8 * 512 * 2)
        o_sqb2 = alloc(8 * 512 * 2)
        o_relu = alloc(2 * 512 * 4)
        o_lnv2 = alloc(512 * 4)
        o_rstd2 = alloc(512 * 4)
        assert off <= ARENA, off

        xT = A.view(o_xT, [8, T], F32)
        hT = A.view(o_hT, [8, PRE + T], BF16)
        cst = A.view(o_cst, [NCST], BF16)
        ident = cst[:, C_ID:C_ID + 128]
        ones_m = cst[:, C_ONES:C_ONES + 128]
        bd_m = cst[:, C_BD:C_BD + 128]
        mask_n = cst[:, C_MN:C_MN + 512]
        mask_f = cst[:, C_MF:C_MF + 512]
        invc = A.view(o_invc, [4, 16], F32)
        smallp = A.view(o_smallp, [L, NSP], F32)
        wslots = [A.view(o_wslot + i * 2048, [8, 128], BF16) for i in range(NSLOT)]
        yattn = A.view(o_yattn, [2, T], BF16)
        ypool = A.view(o_ypool, [4, T], BF16)
        yconv = A.view(o_yconv, [4, T], BF16)
        ktlb = A.view(o_ktlb, [2, 2688], BF16)
        vlb = A.view(o_vlb, [2, NLBBLK, 192], BF16)
        vt = A.view(o_vt, [T], BF16)
        sqs = [A.view(o_sq + i * 1024, [512], BF16) for i in range(2)]
        lnvs = [A.view(o_lnv + i * 2048, [512], F32) for i in range(2)]
        rstds = [A.view(o_rstd + i * 2048, [512], F32) for i in range(2)]
        lnvA = A.view(o_lnvA, [512], F32)
        rstdA = A.view(o_rstdA, [512], F32)
        acc = A.view(o_acc, [2, T], F32)
        ktown = A.view(o_ktown, [T], BF16)
        vown = A.view(o_vown, [16, 192], BF16)
        qtf2 = A.view(o_qtf, [2, T], BF16)
        pts = [A.view(o_pt + i * 1024, [512], BF16) for i in range(4)]
        rtmps = [A.view(o_rtmp + i * 2048, [512], F32) for i in range(2)]
        xlbs = [A.view(o_xlb + i * 8192, [8, 256], F32) for i in range(2)]
        hlbs = [A.view(o_hlb + i * 4096, [8, 256], BF16) for i in range(2)]
        sqb = A.view(o_sqb, [8, 512], BF16)
        hlbs512 = [A.view(o_xlb + i * 8192, [8, 512], BF16) for i in range(2)]
        Ubuf = A.view(o_U, [PRE + T], F32)
        SAl = [A.view(o_SAl + i * 2112, [528], F32) for i in range(2)]
        SBl = [A.view(o_SBl + i * 2112, [528], F32) for i in range(2)]
        Dch = [A.view(o_Dch + i * 1024, [512], BF16) for i in range(2)]
        pm = A.view(o_pm, [4, 128], BF16)
        dtmp = A.view(o_dt, [16], F32)
        UC = A.view(o_UC, [PRE + T], F32)
        CSl = [A.view(o_CSl + i * 2112, [528], F32) for i in range(2)]
        Ych = [A.view(o_Ych + i * 2048, [512], F32) for i in range(2)]
        merged = A.view(o_merged, [8, T], BF16)
        upsl = [A.view(o_upsl + i * 2560, [10, 128], BF16) for i in range(2)]
        sg = A.view(o_sg, [3, 512], F32)
        mt = A.view(o_mt, [512], F32)
        mt2 = A.view(o_mt2, [512], F32)
        w1s = [A.view(o_w1 + i * 16384, [8, 1024], BF16) for i in range(2)]
        w2s = [A.view(o_w2 + i * 16384, [8, 1024], BF16) for i in range(2)]
        fTs = [A.view(o_fT + i * 8192, [8, 512], BF16) for i in range(2)]
        relus = [A.view(o_relu + i * 2048, [512], F32) for i in range(2)]
        sqb2 = A.view(o_sqb2, [8, 512], BF16)
        lnv2 = A.view(o_lnv2, [512], F32)
        rstd2 = A.view(o_rstd2, [512], F32)

        bank_ctr = [0]

        def bank():
            b = ps[bank_ctr[0] % 8][:, :]
            bank_ctr[0] += 1
            return b

        def mm(out, lhsT, rhs, start, stop, skip=False):
            R.add("pe", lambda e, out=out, lhsT=lhsT, rhs=rhs, start=start, stop=stop, skip=skip:
                  e.matmul(out, lhsT, rhs, start=start, stop=stop, skip_group_check=skip),
                  reads=[lhsT, rhs], writes=[out])

        def act(out, in_, func, bias=None, scale=None, extra_reads=()):
            kw = {}
            rd = [in_] + list(extra_reads)
            if bias is not None:
                kw["bias"] = bias
                if not isinstance(bias, float):
                    rd.append(bias)
            if scale is not None:
                kw["scale"] = scale
                if not isinstance(scale, float):
                    rd.append(scale)
            R.add("act", lambda e, out=out, in_=in_, func=func, kw=kw: e.activation(out, in_, func, **kw),
                  reads=rd, writes=[out])

        def dve_tt(out, in0, in1, op):
            R.add("dve", lambda e, out=out, in0=in0, in1=in1, op=op: e.tensor_tensor(out, in0, in1, op),
                  reads=[in0, in1], writes=[out])

        def gp_tt(out, in0, in1, op):
            R.add("pool", lambda e, out=out, in0=in0, in1=in1, op=op: e.tensor_tensor(out, in0, in1, op),
                  reads=[in0, in1], writes=[out])

        def dve_stt(out, in0, scalar, in1, op0, op1):
            rd = [in0, in1] + ([] if isinstance(scalar, float) else [scalar])
            R.add("dve", lambda e, out=out, in0=in0, scalar=scalar, in1=in1, op0=op0, op1=op1:
                  e.scalar_tensor_tensor(out, in0, scalar, in1, op0, op1), reads=rd, writes=[out])

        def dve_ts(out, in0, scalar1, op0):
            rd = [in0] + ([] if isinstance(scalar1, float) else [scalar1])
            R.add("dve", lambda e, out=out, in0=in0, scalar1=scalar1, op0=op0:
                  e.tensor_scalar(out, in0, scalar1, None, op0), reads=rd, writes=[out])

        def dve_copy(out, in_):
            R.add("dve", lambda e, out=out, in_=in_: e.tensor_copy(out, in_), reads=[in_], writes=[out])

        def dve_recip(out, in_):
            R.add("dve", lambda e, out=out, in_=in_: e.reciprocal(out, in_), reads=[in_], writes=[out])

        def dve_memset(out, val):
            R.add("dve", lambda e, out=out, val=val: e.memset(out, val), reads=[], writes=[out])

        def dma(eng, out, in_, reads=(), writes=()):
            R.add(eng, lambda e, out=out, in_=in_: e.dma_start(out=out, in_=in_), reads=list(reads), writes=list(writes), dma=True)

        wslot_ctr = [0]
        prefetched = []

        def _issue_wchunk(src):
            s_ = wslots[wslot_ctr[0] % NSLOT]
            wslot_ctr[0] += 1
            dma("pool", s_, src.rearrange("p (k n) -> p k n", k=8), writes=[s_])
            return s_

        def load_wchunk(src, key=None):
            if prefetched:
                k0, s0_ = prefetched.pop(0)
                assert k0 == key, (k0, key)
                return s0_
            return _issue_wchunk(src)

        wcache = {}

        def load_win(l, ci):
            key = ("win", l, ci)
            if prefetched:
                return load_wchunk(win_d[l, ci], key=key)
            s_ = load_wchunk(win_d[l, ci], key=key)
            wcache[key] = (s_, wslot_ctr[0] - 1)
            return s_

        def load_win_cached(l, ci):
            key = ("win", l, ci)
            if not prefetched and key in wcache and wcache[key][1] is not None and wslot_ctr[0] - wcache[key][1] < NSLOT:
                return wcache[key][0]
            return load_win(l, ci)

        def prefetch_win(l, cis):
            assert not prefetched
            for ci in cis:
                s_ = _issue_wchunk(win_d[l, ci])
                prefetched.append((("win", l, ci), s_))
                wcache[("win", l, ci)] = (s_, wslot_ctr[0] - 1)

        dma("pool", cst, cst_d, writes=[cst])
        dma("sp", invc, invc_d.rearrange("p (g j) -> p g j", g=4), writes=[invc])
        for l in range(L):
            dma("sp", smallp[:, l, :], smallp_d[l], writes=[smallp[:, l, :]])
        def load_x(tc):
            for c in range(8):
                dma("pool", xT[:, c, tc * TCW:(tc + 1) * TCW], xT_d[c * 128:(c + 1) * 128, tc * TCW:(tc + 1) * TCW],
                    writes=[xT[:, c, tc * TCW:(tc + 1) * TCW]])
        load_x(0)

        def spc(l, col):
            return smallp[:, l, col:col + 1]

        def rms_stats(src_fn, ntok, l, sqbuf, lnbuf, rsbuf, flag=False):
            for kc in range(8):
                act(sqbuf[:, kc, 0:ntok], src_fn(kc), AF.Square)
            b = bank()
            for kc in range(8):
                mm(b[:, 0:ntok], ones_m, sqbuf[:, kc, 0:ntok], kc == 0, kc == 7)
            act(lnbuf[:, 0:ntok], b[:, 0:ntok], AF.Ln, bias=spc(l, SP_EPS))
            act(rsbuf[:, 0:ntok], lnbuf[:, 0:ntok], AF.Exp, scale=-0.5)
            if flag:
                dve_ts(rsbuf[:, 0:ntok], rsbuf[:, 0:ntok], spc(l, SP_FLAG), ALU.mult)

        def rms_apply(src_fn, dst_fn, ntok, l, gain_col, rsbuf):
            for kc in range(8):
                dve_stt(dst_fn(kc), src_fn(kc), spc(l, gain_col + kc), rsbuf[:, 0:ntok], ALU.mult, ALU.mult)

        def rms_norm(src_fn, dst_fn, ntok, l, gain_col, sqbuf, lnbuf, rsbuf, flag=False):
            rms_stats(src_fn, ntok, l, sqbuf, lnbuf, rsbuf, flag)
            rms_apply(src_fn, dst_fn, ntok, l, gain_col, rsbuf)

        def proj(b, ncol, wsl, rhs_fn):
            for kc in range(8):
                mm(b, wsl[:, kc, 0:ncol] if ncol != 128 else wsl[:, kc, :], rhs_fn(kc), kc == 0, kc == 7)

        class Pipe:
            NST = 5

            def __init__(self):
                self.items = []

            def _step(self, t):
                n = len(self.items)
                for st in range(self.NST - 1, -1, -1):
                    i = t - st
                    if 0 <= i < n and st in self.items[i]:
                        self.items[i][st]()

            def push(self, stages):
                self.items.append(stages)
                self._step(len(self.items) - 1)

            def flush(self):
                n = len(self.items)
                for t in range(n, n + self.NST - 1):
                    self._step(t)
                self.items = []

        item_ctr = [0]

        def qk_item(l, wsl, rhs_fn, ntok, gain_col, dst_view, fold, split_dst=None):
            k = item_ctr[0] % 2
            item_ctr[0] += 1
            sqk, lnk, rsk = sqs[k], lnvs[k], rstds[k]
            st = {}
            box = {}

            def s0():
                box["b"] = bank()
                proj(box["b"][:, 0:ntok], 128, wsl, rhs_fn)

            def s1():
                act(sqk[:, 0:ntok], box["b"][:, 0:ntok], AF.Square)

            def s2():
                box["b2"] = bank()
                mm(box["b2"][:, 0:ntok], bd_m, sqk[:, 0:ntok], True, True)

            def s3():
                act(lnk[:, 0:ntok], box["b2"][:, 0:ntok], AF.Ln, bias=spc(l, SP_EPS))
                act(rsk[:, 0:ntok], lnk[:, 0:ntok], AF.Exp, scale=-0.5)

            def s3b():
                psb = box["b"]
                if split_dst is None:
                    dve_stt(dst_view, fold(psb[:, 0:ntok]), spc(l, gain_col), fold(rsk[:, 0:ntok]), ALU.mult, ALU.mult)
                else:
                    for a in range(2):
                        pr = slice(64 * a, 64 * a + 64)
                        dve_stt(split_dst(a), fold(psb[pr, 0:ntok]), smallp[pr, l, gain_col:gain_col + 1],
                                fold(rsk[pr, 0:ntok]), ALU.mult, ALU.mult)

            def s34():
                s3()
                s3b()
            return {0: s0, 1: s1, 2: s2, 3: s34}

        def copy_item(wsl, rhs_fn, ntok, dst):
            box = {}

            def s0():
                box["b"] = bank()
                proj(box["b"][:, 0:ntok], 128, wsl, rhs_fn)

            def s1():
                dve_copy(dst, box["b"][:, 0:ntok])
            return {0: s0, 1: s1}

        def vblk_item(src_cols, dstv, nb):
            box = {}

            def s0():
                box["b"] = bank()
                for j, sc in enumerate(src_cols):
                    mm(box["b"][:, j * 128:(j + 1) * 128], sc, ident, True, True)

            def s1():
                srcv = box["b"][:, 0:nb * 128].rearrange("p (b h c) -> p b h c", b=nb, h=2)
                dve_copy(dstv, srcv)
            return {0: s0, 1: s1}

        def layer(l, last):
            pipe = Pipe()
            dma("pool", pm, pmix_d[l].rearrange("p (g n) -> p g n", g=4), writes=[pm])
            dve_memset(vlb[:, :, :, 64:128], 1.0)
            CH = 256 if l == 0 else 512
            NCH = LBT // CH

            def lb_stats(c):
                xb = xlbs[c % 2]
                dma("sp", xb, xlb_d[c].rearrange("p (k n) -> p k n", k=8), writes=[xb])
                rms_stats(lambda kc: xb[:, kc, :], CH, l, sqb, lnvA, rstdA, flag=True)

            def lb_apply(c):
                xb, hb_ = xlbs[c % 2], hlbs[c % 2]
                rms_apply(lambda kc: xb[:, kc, :], lambda kc: hb_[:, kc, :], CH, l, SP_NMIX, rstdA)
                if c == NCH - 1:
                    dve_copy(hT[:, :, 0:PRE], hb_[:, :, CH - PRE:CH])

            pending_apply = None
            for c8 in range(NCH):
                if l == 0:
                    hb = hlbs[c8 % 2]
                    if c8 == 0:
                        lb_stats(0)
                        lb_apply(0)
                    if c8 % 2 == 0:
                        tc = c8 // 2
                        if tc + 1 < NTC:
                            load_x(tc + 1)
                        rms_norm(lambda kc, tc=tc: xT[:, kc, tc * TCW:(tc + 1) * TCW],
                                 lambda kc, tc=tc: hT[:, kc, PRE + tc * TCW:PRE + (tc + 1) * TCW],
                                 TCW, l, SP_NMIX, sqb, lnvA, rstdA)
                    if c8 + 1 < NCH:
                        lb_stats(c8 + 1)
                        pending_apply = c8 + 1
                else:
                    hb = hlbs512[c8 % 2]
                    srcx = xch_out[l - 1][c8].ap()
                    for kc in range(8):
                        dma("sp", hb[:, kc, :], srcx[kc * 128:(kc + 1) * 128, :], reads=[srcx], writes=[hb[:, kc, :]])
                    if c8 == LBT // CH - 1:
                        dve_ts(hT[:, :, 0:PRE], hb[:, :, CH - PRE:CH], spc(l, SP_FLAG), ALU.mult)
                for g in (2, 1, 0):
                    d = DIL[g]
                    lbg = 128 * d
                    lo = max(c8 * CH, LBT - lbg)
                    if lo >= (c8 + 1) * CH:
                        continue
                    n = (c8 + 1) * CH - lo
                    c0 = lo - c8 * CH
                    tpp = lo - (LBT - lbg)
                    kbase = (0, 128, 640)[g]
                    for p in range(2):
                        wk = load_win_cached(l, CI_K + 2 * g + p)
                        kv = ktlb[:, p, kbase:kbase + lbg].rearrange("p (r i) -> p r i", r=d)
                        i0 = tpp // d
                        pipe.push(qk_item(l, wk, lambda kc, c0=c0, n=n, hb=hb: hb[:, kc, c0:c0 + n], n, SP_KG,
                                          kv[:, :, i0:i0 + n // d], lambda a, d=d: a.rearrange("p (i r) -> p r i", r=d)))
                        wv = load_win_cached(l, CI_V + 2 * g + p)
                        pipe.push(copy_item(wv, lambda kc, c0=c0, n=n, hb=hb: hb[:, kc, c0:c0 + n], n,
                                            vtlb_view(g, p)[:, tpp:tpp + n]))
                if l == 0 and pending_apply is not None:
                    lb_apply(pending_apply)
                    pending_apply = None
            pipe.flush()
            if 'B' not in phases:
                return
            for g in range(3):
                d = DIL[g]
                for p in range(2):
                    src = vtlb_view(g, p)
                    for r0 in range(0, d, 4):
                        nb = min(4, d - r0)
                        cols = [src[:, (r0 + j):(r0 + j) + 127 * d + 1:d] if d > 1 else src[:, 0:128] for j in range(nb)]
                        blk0 = LBBLK[g] + r0
                        dstv = vlb[:, p, blk0:blk0 + nb, :].rearrange("p b (h c) -> p b h c", h=3)[:, :, 0:3:2, :]
                        pipe.push(vblk_item(cols, dstv, nb))
            pipe.flush()

            if 'C' not in phases:
                return
            dve_memset(vown[:, :, 64:128], 1.0)
            dve_memset(qtf2[64:128, 0, :], 0.0)
            dve_memset(qtf2[0:64, 1, :], 0.0)
            for p in range(2):
                for g in range(3):
                    d = DIL[g]
                    Lg = T // d
                    NB = Lg // 128
                    kbase = (0, 128, 640)[g]
                    fold = (lambda a, d=d: a.rearrange("p (i r) -> p r i", r=d))
                    hsl = lambda kc, tc: hT[:, kc, PRE + tc * TCW:PRE + (tc + 1) * TCW]
                    wk = load_win(l, CI_K + 2 * g + p)
                    ktv = ktown.rearrange("p (r i) -> p r i", r=d)
                    for tc in range(NTC):
                        i0 = tc * TCW // d
                        pipe.push(qk_item(l, wk, lambda kc, tc=tc: hsl(kc, tc), TCW, SP_KG,
                                          ktv[:, :, i0:i0 + TCW // d], fold))
                    wv = load_win(l, CI_V + 2 * g + p)
                    for tc in range(NTC):
                        pipe.push(copy_item(wv, lambda kc, tc=tc: hsl(kc, tc), TCW, vt[:, tc * TCW:(tc + 1) * TCW]))
                    wq = load_win(l, CI_Q + 2 * g + p)
                    for tc in range(NTC):
                        i0 = tc * TCW // d
                        pipe.push(qk_item(l, wq, lambda kc, tc=tc: hsl(kc, tc), TCW, SP_QG, None, fold,
                                          split_dst=lambda a, i0=i0, d=d: qtf2[64 * a:64 * a + 64, a, :].rearrange(
                                              "p (r i) -> p r i", r=d)[:, :, i0:i0 + TCW // d]))
                    for q4 in range(4):
                        cols = []
                        for j in range(4):
                            blk = q4 * 4 + j
                            r, n = blk // NB, blk % NB
                            s0_ = r + 128 * d * n
                            cols.append(vt[:, s0_:s0_ + 127 * d + 1:d] if d > 1 else vt[:, s0_:s0_ + 128])
                        dstv = vown[:, q4 * 4:q4 * 4 + 4, :].rearrange("p b (h c) -> p b h c", h=3)[:, :, 0:3:2, :]
                        pipe.push(vblk_item(cols, dstv, 4))
                    for bp in range(8):
                        pipe.push(attn_item(l, p, g, bp, d, NB, kbase))
                for tc in range(NTC):
                    pipe.push({3: (lambda tc=tc, p=p: norm_attn_act(tc, p)), 4: (lambda tc=tc, p=p: norm_attn(tc, p))})
            pipe.flush()

        def norm_attn_act(tc, p):
            ts = slice(tc * TCW, (tc + 1) * TCW)
            rt = rtmps[tc % 2]
            act(rt[0:64, :], acc[64:128, 0, ts], AF.Ln)
            act(rt[0:64, :], rt[0:64, :], AF.Exp, scale=-1.0)
            act(rt[64:128, :], acc[0:64, 1, ts], AF.Ln)
            act(rt[64:128, :], rt[64:128, :], AF.Exp, scale=-1.0)

        def norm_attn(tc, p):
            ts = slice(tc * TCW, (tc + 1) * TCW)
            rt = rtmps[tc % 2]
            dve_tt(yattn[0:64, p, ts], acc[0:64, 0, ts], rt[0:64, :], ALU.mult)
            dve_tt(yattn[64:128, p, ts], acc[64:128, 1, ts], rt[64:128, :], ALU.mult)

        def attn_item(l, p, g, bp, d, NB, kbase):
            blks = (2 * bp, 2 * bp + 1)
            box = {}

            def s0():
                box["s"] = []
                for j, blk in enumerate(blks):
                    r, n = blk // NB, blk % NB
                    sb_ = bank()
                    mm(sb_, ident, mask_f if n == 0 else mask_n, True, False, skip=True)
                    for a in range(2):
                        qv = qtf2[:, a, blk * 128:(blk + 1) * 128]
                        if n == 0:
                            kprev = ktlb[:, p, kbase + r * 128:kbase + (r + 1) * 128]
                        else:
                            kprev = ktown[:, (blk - 1) * 128:blk * 128]
                        kcur = ktown[:, blk * 128:(blk + 1) * 128]
                        mm(sb_[:, 256 * a:256 * a + 128], kprev, qv, False, False, skip=True)
                        mm(sb_[:, 256 * a + 128:256 * a + 256], kcur, qv, False, a == 1, skip=True)
                    box["s"].append(sb_)

            def s1():
                box["pt"] = []
                for j, blk in enumerate(blks):
                    pt = pts[(2 * bp + j) % 4]
                    act(pt, box["s"][j], AF.Exp, scale=0.125)
                    box["pt"].append(pt)

            def s2():
                ob = bank()
                box["o"] = ob
                for a in range(2):
                    for j, blk in enumerate(blks):
                        r, n = blk // NB, blk % NB
                        pt = box["pt"][j]
                        if n == 0:
                            vprev = vlb[:, p, LBBLK[g] + r, 64 * a:64 * a + 128]
                        else:
                            vprev = vown[:, blk - 1, 64 * a:64 * a + 128]
                        vcur = vown[:, blk, 64 * a:64 * a + 128]
                        oslot = ob[:, (2 * a + j) * 128:(2 * a + j + 1) * 128]
                        mm(oslot, vprev, pt[:, 256 * a:256 * a + 128], True, False, skip=True)
                        mm(oslot, vcur, pt[:, 256 * a + 128:256 * a + 256], False, True, skip=True)

            def s3():
                ob = box["o"]
                for a in range(2):
                    src = ob[:, 256 * a:256 * a + 256]
                    r, n = blks[0] // NB, blks[0] % NB
                    if g < 2:
                        s0_ = r + 128 * d * n
                        av = acc[:, a, s0_:s0_ + 255 * d + 1:d] if d > 1 else acc[:, a, s0_:s0_ + 256]
                    else:
                        av = acc[:, a, :].rearrange("p (i r) -> p r i", r=16)[:, r:r + 2, :]
                        src = src.rearrange("p (q i) -> p q i", q=2)
                    if g == 0:
                        dve_copy(av, src)
                    else:
                        dve_tt(av, av, src, ALU.add)
            return {0: s0, 1: s1, 2: s2, 3: s3}

        def pool_item(l, g, w, wp, tc):
            k = item_ctr[0] % 2
            item_ctr[0] += 1
            c0 = PRE + tc * TCW
            c1 = c0 + TCW
            LW = TCW + 16
            box = {}

            def s0():
                box["b"] = bank()
                proj(box["b"], 128, wp, lambda kc: hT[:, kc, c0:c1])
                if tc == 0:
                    box["bp"] = bank()
                    proj(box["bp"][:, 0:PRE], 128, wp, lambda kc: hT[:, kc, 0:PRE])

            def s1():
                if tc == 0:
                    act(Ubuf[:, 0:PRE], box["bp"][:, 0:PRE], AF.Copy)
                act(Ubuf[:, c0:c1], box["b"], AF.Copy)

            def s2():
                bufs = [SAl[k], SBl[k]]
                cur, cur_is_u = Ubuf, True
                sh, j = 1, 0
                while sh < w:
                    lo = 16 - (w - 2 * sh)
                    nxt = bufs[j % 2]
                    if cur_is_u:
                        u0 = c0 - 16 + lo
                        dve_tt(nxt[:, lo:LW], Ubuf[:, u0:c1], Ubuf[:, u0 - sh:c1 - sh], ALU.add)
                    else:
                        dve_tt(nxt[:, lo:LW], cur[:, lo:LW], cur[:, lo - sh:LW - sh], ALU.add)
                    cur, cur_is_u = nxt, False
                    sh *= 2
                    j += 1
                dch = Dch[k]
                dve_stt(dch[:, :], cur[:, 16:LW], 1.0 / w, Ubuf[:, c0:c1], ALU.mult, ALU.subtract)
                if tc == 0:
                    dve_tt(dtmp[:, :], cur[:, 16:32], invc[:, g, :], ALU.mult)
                    dve_tt(dch[:, 0:16], dtmp[:, :], Ubuf[:, PRE:PRE + 16], ALU.subtract)

            def s3():
                box["b2"] = bank()
                mm(box["b2"], pm[:, g, :], Dch[k][:, :], True, True)

            def s4():
                act(ypool[:, g, tc * TCW:(tc + 1) * TCW], box["b2"], AF.Identity, scale=spc(l, SP_PSC + g))
            return {0: s0, 1: s1, 2: s2, 3: s3, 4: s4}

        def conv_item(l, c, wc, wx, wb, tc):
            k = item_ctr[0] % 2
            item_ctr[0] += 1
            c0 = PRE + tc * TCW
            c1 = c0 + TCW
            LW = TCW + 16
            box = {}

            def s0():
                box["b1"] = bank()
                proj(box["b1"], 128, wc, lambda kc: hT[:, kc, c0:c1])
                box["b2"] = bank()
                proj(box["b2"], 128, wx, lambda kc: hT[:, kc, c0:c1])
                if tc == 0:
                    box["bp"] = bank()
                    proj(box["bp"][:, 0:PRE], 128, wc, lambda kc: hT[:, kc, 0:PRE])
                    proj(box["bp"][:, 256:256 + PRE], 128, wx, lambda kc: hT[:, kc, 0:PRE])

            def s1():
                if tc == 0:
                    act(CSl[k][:, 0:PRE], box["bp"][:, 0:PRE], AF.Copy)
                act(CSl[k][:, 16:LW], box["b1"], AF.Copy)

            def s2():
                if tc == 0:
                    dve_tt(UC[:, 0:PRE], CSl[k][:, 0:PRE], box["bp"][:, 256:256 + PRE], ALU.mult)
                dve_tt(UC[:, c0:c1], CSl[k][:, 16:LW], box["b2"], ALU.mult)
                y = Ych[k]
                dve_ts(y[:, :], UC[:, c0:c1], spc(l, SP_CW + 2 * 4 + c), ALU.mult)
                dve_stt(y[:, :], UC[:, c0 - 1:c1 - 1], spc(l, SP_CW + 1 * 4 + c), y[:, :], ALU.mult, ALU.add)
                dve_stt(y[:, :], UC[:, c0 - 2:c1 - 2], spc(l, SP_CW + 0 * 4 + c), y[:, :], ALU.mult, ALU.add)

            def s3():
                box["b3"] = bank()
                proj(box["b3"], 128, wb, lambda kc: hT[:, kc, c0:c1])

            def s4():
                dve_tt(yconv[:, c, tc * TCW:(tc + 1) * TCW], Ych[k][:, :], box["b3"], ALU.mult)
            return {0: s0, 1: s1, 2: s2, 3: s3, 4: s4}

        def layer_rest(l, last):
            if 'D' not in phases:
                return
            pipe = Pipe()
            for g in range(4):
                wp = load_win(l, CI_POOL + g)
                for tc in range(NTC):
                    pipe.push(pool_item(l, g, POOLW[g], wp, tc))
            pipe.flush()

            if 'E' not in phases:
                return
            for c in range(4):
                wc = load_win(l, CI_CC + c)
                wx = load_win(l, CI_CX + c)
                wb = load_win(l, CI_CB + c)
                for tc in range(NTC):
                    pipe.push(conv_item(l, c, wc, wx, wb, tc))
            pipe.flush()

            if debug and l == 0:
                dbg_dump("d_yattn", yattn, 2)
                dbg_dump("d_ypool", ypool, 4)
                dbg_dump("d_yconv", yconv, 4)
                dbg_dump("d_h", hT[:, :, PRE:PRE + T], 8)

            if 'F' not in phases:
                return
            for fc in range(8):
                us = upsl[fc % 2]
                dma("pool", us, ups_d[l, fc].rearrange("p (k n) -> p k n", k=10), writes=[us])
                wg = [load_win(l, CI_G + br * 8 + fc) for br in range(3)]
                for tc in range(NTC):
                    ts = slice(tc * TCW, (tc + 1) * TCW)
                    hs = slice(PRE + tc * TCW, PRE + (tc + 1) * TCW)
                    gb = []
                    for br in range(3):
                        b = bank()
                        proj(b, 128, wg[br], lambda kc, hs=hs: hT[:, kc, hs])
                        gb.append(b)
                    ub = []
                    for br, (src, nk, k0) in enumerate(((ypool, 4, 0), (yconv, 4, 4), (yattn, 2, 8))):
                        b = bank()
                        for k in range(nk):
                            mm(b, us[:, k0 + k, :], src[:, k, ts], k == 0, k == nk - 1)
                        ub.append(b)
                    for br in range(3):
                        act(sg[:, br, :], gb[br], AF.Sigmoid, bias=spc(l, SP_BG + br * 8 + fc))
                    dve_tt(mt[:, :], sg[:, 0, :], ub[0], ALU.mult)
                    dve_tt(mt2[:, :], sg[:, 1, :], ub[1], ALU.mult)
                    dve_tt(mt[:, :], mt[:, :], mt2[:, :], ALU.add)
                    dve_tt(mt2[:, :], sg[:, 2, :], ub[2], ALU.mult)
                    dve_tt(merged[:, fc, ts], mt[:, :], mt2[:, :], ALU.add)
            if debug and l == 0:
                dbg_dump("d_merged", merged, 8)
            for fo in range(8):
                wo = load_wchunk(wo_d[l, fo])
                for tc in range(NTC):
                    ts = slice(tc * TCW, (tc + 1) * TCW)
                    b = bank()
                    for kc in range(8):
                        mm(b, wo[:, kc, :], merged[:, kc, ts], kc == 0, kc == 7)
                    dve_tt(xT[:, fo, ts], xT[:, fo, ts], b, ALU.add)
            if debug and l == 0:
                dbg_dump_f32("d_x1", xT, 8)

            if 'G' not in phases:
                return
            def ffn_norm(tc):
                rms_norm(lambda kc: xT[:, kc, tc * TCW:(tc + 1) * TCW],
                         lambda kc: hT[:, kc, PRE + tc * TCW:PRE + (tc + 1) * TCW],
                         TCW, l, SP_NMLP, sqb2, lnv2, rstd2)
            ffn_norm(0)
            for q in range(4):
                if q == 3 and not last:
                    prefetch_win(l + 1, [CI_K + 4, CI_V + 4, CI_K + 5, CI_V + 5])
                w1 = w1s[q % 2]
                w2 = w2s[q % 2]
                for j in range(4):
                    dma("pool", w1[:, :, j * 256:(j + 1) * 256],
                        w1_d[l, q].rearrange("p (k n) -> p k n", k=8)[:, :, j * 256:(j + 1) * 256],
                        writes=[w1[:, :, j * 256:(j + 1) * 256]])
                for j in range(4):
                    dma("pool", w2[:, :, j * 256:(j + 1) * 256],
                        w2_d[l, q].rearrange("p (k n) -> p k n", k=8)[:, :, j * 256:(j + 1) * 256],
                        writes=[w2[:, :, j * 256:(j + 1) * 256]])
                for tc in range(NTC):
                    if q == 0 and tc + 1 < NTC:
                        ffn_norm(tc + 1)
                    ts = slice(tc * TCW, (tc + 1) * TCW)
                    hs = slice(PRE + tc * TCW, PRE + (tc + 1) * TCW)
                    fT = fTs[tc % 2]
                    for ffc in range(8):
                        b = bank()
                        for kc in range(8):
                            mm(b, w1[:, kc, ffc * 128:(ffc + 1) * 128], hT[:, kc, hs], kc == 0, kc == 7)
                        rl = relus[ffc % 2]
                        act(rl[:, :], b, AF.Relu)
                        dve_tt(fT[:, ffc, :], rl[:, :], b, ALU.mult)
                    for fo in range(8):
                        b = bank()
                        for ffc in range(8):
                            mm(b, w2[:, ffc, fo * 128:(fo + 1) * 128], fT[:, ffc, :], ffc == 0, ffc == 7)
                        dve_tt(xT[:, fo, ts], xT[:, fo, ts], b, ALU.add)
                    if q == 3 and not last:
                        rms_norm(lambda kc: xT[:, kc, ts], lambda kc: hT[:, kc, hs],
                                 TCW, l + 1, SP_NMIX, sqb2, lnv2, rstd2)
                        xi = xch_in[l][tc].ap()
                        xo = xch_out[l][tc].ap()
                        for kc in range(8):
                            dma("sp", xi[kc * 128:(kc + 1) * 128, :], hT[:, kc, hs], reads=[hT[:, kc, hs]], writes=[xi[kc * 128:(kc + 1) * 128, :]])
                        R.add("pool", lambda e, xi=xi, xo=xo: e.collective_compute(
                            "AllGather", ALU.bypass, replica_groups=[[0, 1], [2, 3], [4, 5], [6, 7]],
                            ins=[xi.opt()], outs=[xo.opt()]), reads=[xi], writes=[xo], cc=True)
                    if q == 3 and last:
                        for kc in range(8):
                            dma("sp", out_d[kc * 128:(kc + 1) * 128, ts], xT[:, kc, ts], reads=[xT[:, kc, ts]])

        vtlb_off = {}
        o_tmp = o_vtlb_small
        for g in range(2):
            for p in range(2):
                vtlb_off[(g, p)] = o_tmp
                o_tmp += 128 * DIL[g] * 2
        for p in range(2):
            vtlb_off[(2, p)] = o_yattn + p * 4096

        def vtlb_view(g, p):
            return A.view(vtlb_off[(g, p)], [128 * DIL[g]], BF16)

        dbg_stage = A.view(o_dbg, [T], F32) if debug else None

        def dbg_dump(name, src3, nch):
            for c in range(nch):
                act(dbg_stage[:, :], src3[:, c, :], AF.Copy)
                dma("sp", dbg[name][c * 128:(c + 1) * 128, :], dbg_stage[:, :], reads=[dbg_stage[:, :]])

        def dbg_dump_f32(name, src3, nch):
            for c in range(nch):
                dma("sp", dbg[name][c * 128:(c + 1) * 128, :], src3[:, c, :], reads=[src3[:, c, :]])

        for l in range(L):
            layer(l, l == L - 1)
            layer_rest(l, l == L - 1)

        R.emit(nc, sems, dma_sems)
    return nc


def _prep_weights(layers, w_in, pool_mix, w_pool_up, w_conv_out, w_attn_up, w_o, w_ff1, w_ff2):
    nl = len(layers)
    win = np.empty((nl, NWC, 128, NKC * 128), np.float32)
    pmix = np.empty((nl, 128, 4 * 128), np.float32)
    ups = np.empty((nl, 8, 128, 10 * 128), np.float32)
    wo = np.empty((nl, 8, 128, NKC * 128), np.float32)
    w1 = np.empty((nl, 4, 128, NKC * 1024), np.float32)
    w2 = np.empty((nl, 4, 128, 8 * 1024), np.float32)
    for i, l in enumerate(layers):
        win[i] = w_in[l].reshape(NKC, 128, NWC, 128).transpose(2, 1, 0, 3).reshape(NWC, 128, NKC * 128)
        pmix[i] = pool_mix[l].transpose(1, 0, 2).reshape(128, 512)
        cat = np.concatenate([w_pool_up[l], w_conv_out[l], w_attn_up[l]], axis=0)
        ups[i] = cat.reshape(10, 128, 8, 128).transpose(2, 1, 0, 3).reshape(8, 128, 1280)
        wo[i] = w_o[l].reshape(NKC, 128, 8, 128).transpose(2, 1, 0, 3).reshape(8, 128, NKC * 128)
        w1[i] = w_ff1[l].reshape(NKC, 128, 4, 1024).transpose(2, 1, 0, 3).reshape(4, 128, NKC * 1024)
        w2[i] = w_ff2[l].reshape(4, 8, 128, 1024).transpose(0, 2, 1, 3).reshape(4, 128, 8 * 1024)
    return dict(win=win, pmix=pmix, ups=ups, wo=wo, w1=w1, w2=w2)


def _prep_small(layers, is_second_half, norm_mix, b_gate, pool_scale, conv_w, q_gain, k_gain, norm_mlp):
    nl = len(layers)
    sp = np.zeros((nl, 128, NSP), np.float32)
    for i, l in enumerate(layers):
        sp[i, :, SP_NMIX:SP_NMIX + 8] = norm_mix[l].reshape(8, 128).T
        sp[i, :, SP_NMLP:SP_NMLP + 8] = norm_mlp[l].reshape(8, 128).T
        sp[i, :, SP_BG:SP_BG + 24] = b_gate[l].reshape(24, 128).T
        sp[i, :, SP_PSC:SP_PSC + 4] = pool_scale[l].reshape(4, 128).T
        sp[i, :, SP_CW:SP_CW + 12] = conv_w[l].reshape(3, 4, 128).transpose(2, 0, 1).reshape(128, 12)
        sp[i, :, SP_QG] = np.tile(q_gain[l], 2)
        sp[i, :, SP_KG] = np.tile(k_gain[l], 2)
        sp[i, :, SP_FLAG] = 1.0 if is_second_half else 0.0
        sp[i, :, SP_EPS] = EPS
    return sp


def _consts(is_second_half):
    cst = np.zeros((128, NCST), np.float32)
    cst[:, C_ID:C_ID + 128] = np.eye(128, dtype=np.float32)
    cst[:, C_ONES:C_ONES + 128] = 1.0 / 1024.0
    bd = np.zeros((128, 128), np.float32)
    bd[0:64, 0:64] = 1.0 / 64.0
    bd[64:128, 64:128] = 1.0 / 64.0
    cst[:, C_BD:C_BD + 128] = bd
    k = np.arange(128)[:, None]
    q = np.arange(128)[None, :]
    mprev = np.where(k >= q, 0.0, NEG).astype(np.float32)
    mcur = np.where(k <= q, 0.0, NEG).astype(np.float32)
    mn = np.concatenate([mprev, mcur], axis=1)
    mfirst = mn if is_second_half else np.concatenate([np.full((128, 128), NEG, np.float32), mcur], axis=1)
    cst[:, C_MN:C_MN + 512] = np.concatenate([mn, mn], axis=1)
    cst[:, C_MF:C_MF + 512] = np.concatenate([mfirst, mfirst], axis=1)
    invc = np.zeros((128, 64), np.float32)
    for g, w in enumerate(POOLW):
        for j in range(16):
            cnt = w if is_second_half else min(j + 1, w)
            invc[:, g * 16 + j] = 1.0 / cnt
    return cst, invc


_NC_CACHE = {}


def _get_nc(n_layers, debug=False):
    key = (n_layers, debug)
    if key not in _NC_CACHE:
        _NC_CACHE[key] = build_program(n_layers, debug)
    return _NC_CACHE[key]


def _run_layers(x_cur, layers, params, debug=False):
    B = x_cur.shape[0]
    nc = _get_nc(len(layers), debug)
    wts = _prep_weights(layers, params["w_in"], params["pool_mix"], params["w_pool_up"], params["w_conv_out"],
                        params["w_attn_up"], params["w_o"], params["w_ff1"], params["w_ff2"])
    in_maps = []
    for core in range(N_CORES):
        b, half = core // 2, core % 2
        xo = np.ascontiguousarray(x_cur[b, half * T:(half + 1) * T, :].T)
        if half == 1:
            xl = np.ascontiguousarray(x_cur[b, 0:T, :].T)
        else:
            xl = np.zeros((D, LBT), np.float32)
        xl = np.ascontiguousarray(xl.reshape(NKC, 128, LBT // 256, 256).transpose(2, 1, 0, 3)).reshape(LBT // 256, 128, NKC * 256)
        cst, invc = _consts(half == 1)
        sp = _prep_small(layers, half == 1, params["norm_mix"], params["b_gate"], params["pool_scale"],
                         params["conv_w"], params["q_gain"], params["k_gain"], params["norm_mlp"])
        m = dict(xT=xo, xlbT=xl, smallp=sp, cst=cst, invc=invc)
        m.update(wts)
        in_maps.append(m)
    res = run_bass_kernel_spmd(nc, in_maps, core_ids=list(range(N_CORES)))
    out = np.empty_like(x_cur)
    for core in range(N_CORES):
        b, half = core // 2, core % 2
        out[b, half * T:(half + 1) * T, :] = res.results[core]["outT"].T
    return out, res


def kernel(x, norm_mix, w_in, b_gate, pool_mix, pool_scale, conv_w, q_gain, k_gain,
           w_pool_up, w_conv_out, w_attn_up, w_o, norm_mlp, w_ff1, w_ff2):
    params = dict(norm_mix=norm_mix, w_in=w_in, b_gate=b_gate, pool_mix=pool_mix, pool_scale=pool_scale,
                  conv_w=conv_w, q_gain=q_gain, k_gain=k_gain, w_pool_up=w_pool_up, w_conv_out=w_conv_out,
                  w_attn_up=w_attn_up, w_o=w_o, norm_mlp=norm_mlp, w_ff1=w_ff1, w_ff2=w_ff2)
    params = {k: np.asarray(v, dtype=np.float32) for k, v in params.items()}
    cur = np.asarray(x, dtype=np.float32)
    out, _ = _run_layers(cur, list(range(DEPTH)), params)
    return out
```
